# Optimizing a Trainium2 kernel written in Bass

```python
import jax, jax.numpy as jnp
from jax import lax
import numpy as np

D_MODEL = 2048
BATCH = 4
SEQ = 2048
DEPTH = 1

N_ATTN_HEADS = 8
HEAD_DIM = 128
ATTN_WIDTH = N_ATTN_HEADS * HEAD_DIM
POOL_WIDTH = D_MODEL - ATTN_WIDTH
POOL_WINDOWS = (2, 4, 8, 16)
N_POOL_GROUPS = len(POOL_WINDOWS)
POOL_GROUP_WIDTH = POOL_WIDTH // N_POOL_GROUPS
IN_WIDTH = 3 * ATTN_WIDTH + POOL_WIDTH
MOBA_BLOCK = 256
MOBA_TOPK = 3
Q_CHUNK = 32
ROPE_THETA = 10000.0
D_FF = -(-8 * D_MODEL // (3 * 256)) * 256
N_MOD = 6
EPS = 1e-6

kernel_name = "hymba_moba_pool_hybrid_layer"


def rmsnorm(x, g):
    xf = x.astype(jnp.float32)
    y = xf * lax.rsqrt(jnp.mean(xf * xf, axis=-1, keepdims=True) + EPS)
    return (y * g.astype(jnp.float32)).astype(x.dtype)


def rope(x, positions):
    dh = x.shape[-1]
    inv_freq = ROPE_THETA ** (-jnp.arange(0, dh, 2, dtype=jnp.float32) / dh)
    ang = positions[:, None, :, None].astype(jnp.float32) * inv_freq
    cos, sin = jnp.cos(ang), jnp.sin(ang)
    xf = x.astype(jnp.float32)
    x1, x2 = xf[..., : dh // 2], xf[..., dh // 2:]
    out = jnp.concatenate([x1 * cos - x2 * sin, x2 * cos + x1 * sin], axis=-1)
    return out.astype(x.dtype)


def moba_attention(q, k, v):
    B, H, S, Dh = q.shape
    nb = -(-S // MOBA_BLOCK)
    pad = nb * MOBA_BLOCK - S
    kb = jnp.pad(k, ((0, 0), (0, 0), (0, pad), (0, 0))).reshape(B, H, nb, MOBA_BLOCK, Dh)
    vb = jnp.pad(v, ((0, 0), (0, 0), (0, pad), (0, 0))).reshape(B, H, nb, MOBA_BLOCK, Dh)
    kmean = jnp.mean(kb.astype(jnp.float32), axis=3)
    kk = min(MOBA_TOPK, nb)
    scale = Dh ** -0.5
    bi = jnp.arange(B)[:, None, None, None]
    hi = jnp.arange(H)[None, :, None, None]
    blk_ids = jnp.arange(nb)

    def chunk(ci):
        start = ci * Q_CHUNK
        qc = lax.dynamic_slice_in_dim(q, start, Q_CHUNK, axis=2)
        t = start + jnp.arange(Q_CHUNK)
        own = start // MOBA_BLOCK
        gate = jnp.einsum('bhqd,bhnd->bhqn', qc.astype(jnp.float32), kmean)
        gate = jnp.where(blk_ids < own, gate, -jnp.inf)
        _, idx = lax.top_k(gate, kk)
        valid = idx < own
        ksel = kb[bi, hi, idx]
        vsel = vb[bi, hi, idx]
        s_sel = jnp.einsum('bhqd,bhqkjd->bhqkj', qc, ksel).astype(jnp.float32) * scale
        s_sel = jnp.where(valid[..., None], s_sel, -jnp.inf).reshape(B, H, Q_CHUNK, kk * MOBA_BLOCK)
        kown = lax.dynamic_index_in_dim(kb, own, axis=2, keepdims=False)
        vown = lax.dynamic_index_in_dim(vb, own, axis=2, keepdims=False)
        s_own = jnp.einsum('bhqd,bhjd->bhqj', qc, kown).astype(jnp.float32) * scale
        kpos = own * MOBA_BLOCK + jnp.arange(MOBA_BLOCK)
        s_own = jnp.where(kpos[None, :] <= t[:, None], s_own, -jnp.inf)
        p = jax.nn.softmax(jnp.concatenate([s_sel, s_own], axis=-1), axis=-1)
        p_sel = p[..., : kk * MOBA_BLOCK].reshape(B, H, Q_CHUNK, kk, MOBA_BLOCK).astype(v.dtype)
        p_own = p[..., kk * MOBA_BLOCK:].astype(v.dtype)
        return (jnp.einsum('bhqkj,bhqkjd->bhqd', p_sel, vsel)
                + jnp.einsum('bhqj,bhjd->bhqd', p_own, vown))

    out = lax.map(chunk, jnp.arange(S // Q_CHUNK))
    return out.transpose(1, 2, 0, 3, 4).reshape(B, H, S, Dh)


def pool_mixer(u, w_pool, pool_scale):
    B, S, _ = u.shape
    uf = u.astype(jnp.float32)
    cs = jnp.pad(jnp.cumsum(uf, axis=1), ((0, 0), (1, 0), (0, 0)))
    t = jnp.arange(S)
    outs = []
    for g, w in enumerate(POOL_WINDOWS):
        sl = slice(g * POOL_GROUP_WIDTH, (g + 1) * POOL_GROUP_WIDTH)
        lo = jnp.maximum(t + 1 - w, 0)
        win_sum = cs[:, 1:, sl] - cs[:, lo, sl]
        cnt = (t + 1 - lo).astype(jnp.float32)
        pooled = (win_sum / cnt[None, :, None] - uf[..., sl]).astype(u.dtype)
        outs.append(jnp.einsum('bsc,cd->bsd', pooled, w_pool[g]))
    return jnp.concatenate(outs, axis=-1) * pool_scale


def setup_inputs(seed: int = 0) -> dict:
    key = jax.random.key(seed)
    ks = jax.random.split(key, 18)
    f32 = jnp.float32
    nrm = lambda k, shape, s: jax.random.normal(k, shape, f32) * s
    x = jax.random.normal(ks[0], (BATCH, SEQ, D_MODEL), f32)
    c = jax.random.normal(ks[1], (BATCH, D_MODEL), f32)
    offs = jax.random.randint(ks[2], (BATCH, 1), 0, 1024, dtype=jnp.int32)
    positions = offs + jnp.arange(SEQ, dtype=jnp.int32)[None, :]
    return {
        "x": x,
        "c": c,
        "positions": positions,
        "w_ada": nrm(ks[3], (DEPTH, D_MODEL, N_MOD * D_MODEL), 0.5 * D_MODEL ** -0.5),
        "b_ada": nrm(ks[4], (DEPTH, N_MOD * D_MODEL), 0.02),
        "g_mix_norm": 1.0 + nrm(ks[5], (DEPTH, D_MODEL), 0.02),
        "w_in": nrm(ks[6], (DEPTH, D_MODEL, IN_WIDTH), D_MODEL ** -0.5),
        "g_q": 1.0 + nrm(ks[7], (DEPTH, HEAD_DIM), 0.02),
        "g_k": 1.0 + nrm(ks[8], (DEPTH, HEAD_DIM), 0.02),
        "w_pool": nrm(ks[9], (DEPTH, N_POOL_GROUPS, POOL_GROUP_WIDTH, POOL_GROUP_WIDTH), POOL_GROUP_WIDTH ** -0.5),
        "pool_scale": 1.0 + nrm(ks[10], (DEPTH, POOL_WIDTH), 0.1),
        "w_out": nrm(ks[11], (DEPTH, D_MODEL, D_MODEL), D_MODEL ** -0.5),
        "g_ffn_norm": 1.0 + nrm(ks[12], (DEPTH, D_MODEL), 0.02),
        "w_gate": nrm(ks[13], (DEPTH, D_MODEL, D_FF), D_MODEL ** -0.5),
        "w_up": nrm(ks[14], (DEPTH, D_MODEL, D_FF), D_MODEL ** -0.5),
        "w_down": nrm(ks[15], (DEPTH, D_FF, D_MODEL), D_FF ** -0.5),
    }


def reference(x, c, positions, w_ada, b_ada, g_mix_norm, w_in, g_q, g_k, w_pool,
              pool_scale, w_out, g_ffn_norm, w_gate, w_up, w_down):
    B, S, D = x.shape
    for l in range(DEPTH):
        mod = jnp.einsum('bd,de->be', jax.nn.silu(c), w_ada[l]) + b_ada[l]
        sh1, sc1, gt1, sh2, sc2, gt2 = [m[:, None, :] for m in jnp.split(mod, N_MOD, axis=-1)]

        h = rmsnorm(x, g_mix_norm[l]) * (1.0 + sc1) + sh1
        z = jnp.einsum('bsd,de->bse', h, w_in[l])
        q, k, v, u = jnp.split(z, [ATTN_WIDTH, 2 * ATTN_WIDTH, 3 * ATTN_WIDTH], axis=-1)
        to_heads = lambda a: a.reshape(B, S, N_ATTN_HEADS, HEAD_DIM)
        q = rope(rmsnorm(to_heads(q), g_q[l]).transpose(0, 2, 1, 3), positions)
        k = rope(rmsnorm(to_heads(k), g_k[l]).transpose(0, 2, 1, 3), positions)
        v = to_heads(v).transpose(0, 2, 1, 3)
        o_attn = moba_attention(q, k, v).transpose(0, 2, 1, 3).reshape(B, S, ATTN_WIDTH)
        o_pool = pool_mixer(u, w_pool[l], pool_scale[l])
        y = jnp.einsum('bse,ed->bsd', jnp.concatenate([o_attn, o_pool], axis=-1), w_out[l])
        x = x + gt1 * y

        h2 = rmsnorm(x, g_ffn_norm[l]) * (1.0 + sc2) + sh2
        a = jnp.einsum('bsd,df->bsf', h2, w_gate[l])
        b = jnp.einsum('bsd,df->bsf', h2, w_up[l])
        f = jnp.einsum('bsf,fd->bsd', jax.nn.silu(a) * b, w_down[l])
        x = x + gt2 * f
    return x
```

```python
import numpy as np
from contextlib import ExitStack
import concourse.bass as bass
import concourse.mybir as mybir
from concourse.bass_utils import run_bass_kernel_spmd

F32 = mybir.dt.float32
BF16 = mybir.dt.bfloat16
I32 = mybir.dt.int32
AF = mybir.ActivationFunctionType
ALU = mybir.AluOpType
AX = mybir.AxisListType

D = 2048
SEQ = 2048
NB = 4
H = 8
DH = 128
DFF = 5632
EPS = 1e-6
NEG = -30000.0
POOL_WINDOWS = (2, 4, 8, 16)
OWNS = [[0, 3, 4, 7], [1, 2, 5, 6]]
TWO_PI = float(2 * np.pi)
C1 = 6.28125
C2 = float(2 * np.pi - 6.28125)

O_C, O_BADA, O_GMIX, O_GFFN, O_GQ, O_GK, O_PSC, O_IFR, O_ALW, O_DIAG, O_HV, O_ID = (
    0, 16, 112, 128, 144, 272, 400, 408, 472, 504, 536, 600)
NCST = 728
OB_ID, OB_TRI, OB_EOH = 0, 128, 640
NCSTB = 1664

DEBUG = ()


class Res:
    __slots__ = ("name", "ws", "rs", "ers")

    def __init__(self, name):
        self.name = name
        self.ws = {}
        self.rs = {}
        self.ers = {}


class Sched:
    ENGS = ("pe", "act", "dve", "pool", "sp")

    def __init__(self, nc, stack):
        self.nc = nc
        self.stack = stack
        self.sems = {}
        self.cnt = {}
        self.waited = {e: {} for e in self.ENGS}
        self.prog = {e: [] for e in self.ENGS}
        for e in self.ENGS:
            self.newsem("E_" + e)

    def newsem(self, key):
        self.sems[key] = self.stack.enter_context(self.nc.semaphore(key))
        self.cnt[key] = 0

    def op(self, eng, fn, reads=(), writes=(), awrites=(), dma=None, ndma=1):
        deps = {}

        def add(d):
            for k, v in d.items():
                if deps.get(k, 0) < v:
                    deps[k] = v
        for r in reads:
            add(r.ws)
        for w in writes:
            add(w.ws)
            add(w.rs)
        for a in awrites:
            add(a.rs)
            add(a.ers)
        waits = []
        wd = self.waited[eng]
        for k, v in deps.items():
            if eng == "pe" and k == "E_pe":
                continue
            if wd.get(k, 0) < v:
                wd[k] = v
                waits.append((self.sems[k], v))
        if dma is not None:
            key, inc = dma, 16
            self.cnt[key] += 16 * ndma
        else:
            key, inc = "E_" + eng, 1
            self.cnt[key] += 1
        tok = (key, self.cnt[key])
        semh = self.sems[key]

        def run(e, waits=waits, fn=fn, semh=semh, inc=inc):
            for (s, v) in waits:
                e.wait_ge(s, v)
            r = fn(e)
            if isinstance(r, (list, tuple)):
                for ins in r:
                    ins.then_inc(semh, inc)
            else:
                r.then_inc(semh, inc)
        self.prog[eng].append(run)
        for r in reads:
            if r.rs.get(tok[0], 0) < tok[1]:
                r.rs[tok[0]] = tok[1]
        for w in writes:
            w.ers = dict(w.rs)
            w.ws = {tok[0]: tok[1]}
            w.rs = {}
        for a in awrites:
            if a.ws.get(tok[0], 0) < tok[1]:
                a.ws[tok[0]] = tok[1]
        return tok

    def barrier(self, engs=("pe", "act", "dve", "sp")):
        for eng in engs:
            waits = []
            wd = self.waited[eng]
            for k, v in self.cnt.items():
                if v == 0 or k == "E_" + eng:
                    continue
                if k.startswith("D_ring") or k == "D_out":
                    continue
                if wd.get(k, 0) < v:
                    wd[k] = v
                    waits.append((self.sems[k], v))

            def run(e, waits=waits):
                for (s, v) in waits:
                    e.wait_ge(s, v)
            self.prog[eng].append(run)

    def final_wait(self, eng, key):
        semh, v = self.sems[key], self.cnt[key]
        self.prog[eng].append(lambda e: e.wait_ge(semh, v))


def build_program(debug=()):
    nc = bass.Bass("TRN2", target_bir_lowering=False)

    def din(name, shape, dt=F32):
        return nc.dram_tensor(name, list(shape), dt, kind="ExternalInput").ap()
    x_own = din("x_own", [1024, D])
    x_oth = din("x_oth", [1024, D])
    x_halo = din("x_halo", [64, D])
    cst_d = din("cst", [128, NCST])
    cstb_d = din("cstb", [128, NCSTB])
    pos_d = din("pos_l", [128, 16], I32)
    invcnt_d = din("invcnt_b", [4, 128, 1024])
    w_ada = din("w_ada", [D, 6 * D])
    w_in = din("w_in", [D, 4096])
    w_pool = din("w_pool", [4, 256, 256])
    w_out = din("w_out", [D, D])
    w_gate = din("w_gate", [D, DFF])
    w_up = din("w_up", [D, DFF])
    w_down = din("w_down", [DFF, D])
    out_d = nc.dram_tensor("out", [1024, D], F32, kind="ExternalOutput").ap()
    dbg = {}

    def dbg_out(name, shape, dt=F32):
        dbg[name] = nc.dram_tensor("dbg_" + name, list(shape), dt, kind="ExternalOutput").ap()
        return dbg[name]

    base = (nc.sbuf_base + 63) // 64 * 64
    avail = nc.sbuf_top - base

    def sb(name, shape, dt, off):
        assert off % 32 == 0, (name, off)
        esz = 4 if dt in (F32, I32) else 2
        n = 1
        for s_ in shape[1:]:
            n *= s_
        assert off + n * esz <= avail, (name, off, n * esz, avail)
        return nc.alloc_sbuf_tensor_at(name, list(shape), dt, offset=base + off)

    ring = [sb("ring0", [128, 16, 512], BF16, 0), sb("ring1", [128, 16, 512], BF16, 16384)]
    N_A3 = 8
    o = 32768
    cst = sb("cst", [128, NCST], F32, o); o += NCST * 4
    cstb = sb("cstb", [128, NCSTB], BF16, o); o += NCSTB * 2
    wpool = sb("wpool", [128, 4, 2, 256], BF16, o); o += 4096
    ones_bf = sb("ones_bf", [128, 128], BF16, o); o += 256
    ones_f = sb("ones_f", [128, 128], F32, o); o += 512
    posi = sb("posi", [128, 16], I32, o); o += 64
    posf = sb("posf", [128, 16], F32, o); o += 64
    s_f = sb("s_f", [128, 16], F32, o); o += 64
    s_bf = sb("s_bf", [128, 16], BF16, o); o += 64
    modT = sb("modT", [128, 96], F32, o); o += 384
    gm1T = sb("gm1T", [128, 16], F32, o); o += 64
    gm2T = sb("gm2T", [128, 16], F32, o); o += 64
    one_f = sb("one_f", [1, 1], F32, o); o += 64
    kmean_bf = sb("kmean_bf", [128, 32], BF16, o); o += 64
    km_f = sb("km_f", [128, 32], F32, o); o += 128
    ssq = sb("ssq", [128, 32], F32, o); o += 128
    rsq = sb("rsq", [128, 32], F32, o); o += 128
    rstd = sb("rstd", [128, 32], F32, o); o += 128
    pbias = sb("pbias", [128, 32], F32, o); o += 128
    st4 = sb("st4", [128, 16], F32, o); o += 64
    st4b = sb("st4b", [128, 16], F32, o); o += 64
    jq0 = sb("jq0", [128, 512], BF16, o); o += 1024
    assert o <= 47104, o
    BCo = 47104
    bc = sb("bc", [128, 2048], F32, BCo)
    rowst = sb("rowst", [1, 512], F32, BCo)
    PH = 55296
    hT = sb("hT", [128, 16, 2112], BF16, PH)
    ring.append(sb("ring2", [128, 16, 512], BF16, PH))
    concatT = sb("concatT", [128, 16, 1024], BF16, PH + 67584)
    cosT = sb("cosT", [128, 16, 64], F32, PH + 100352)
    sinT = sb("sinT", [128, 16, 64], F32, PH + 104448)
    XR = PH + 108544
    QT = sb("QT", [128, 4, 1024], BF16, XR)
    KT = sb("KT", [128, 4, 2048], BF16, XR + 8192)
    Vt = sb("Vt", [128, 16, 512], BF16, XR + 24576)
    SC = XR + 40960
    sq = sb("sq", [128, 512], F32, SC)
    qn = sb("qn", [128, 512], F32, SC + 2048)
    ta = sb("ta", [128, 512], F32, SC + 4096)
    qrb = sb("qrb", [128, 512], BF16, SC + 6144)
    sq2 = sb("sq2", [128, 512], F32, BCo)
    qn2 = sb("qn2", [128, 512], F32, BCo + 2048)
    ta2 = sb("ta2", [128, 512], F32, BCo + 4096)
    qrb2 = sb("qrb2", [128, 512], BF16, BCo + 6144)
    jq1 = sb("jq1", [128, 512], BF16, BCo + 7168)
    jq = [jq0, jq1]
    biasT = [sb("biasT0", [32, 1024], BF16, SC), sb("biasT1", [32, 1024], BF16, SC + 2048)]
    PT = [sb("PT0", [128, 512], BF16, SC + 4096), sb("PT1", [128, 512], BF16, SC + 5120)]
    rden = sb("rden", [128, 512], F32, BCo + 2048)
    PT2 = sb("PT2", [128, 512], BF16, BCo + 4096)
    gmw = sb("gmw", [128, 64], F32, SC + 7168)
    top8 = sb("top8", [128, 64], F32, SC + 7424)
    selw = sb("selw", [128, 64], F32, SC + 7680)
    xb = [sb("xb0", [128, 2048], F32, PH + 67584), sb("xb1", [128, 2048], F32, PH + 67584 + 8192)]
    xnb = [sb("xn0", [128, 2048], F32, XR + 29696), sb("xn1", [128, 2048], F32, XR + 37888)]
    junk1 = sb("junk1", [128, 2048], BF16, BCo)
    ang = sb("ang", [128, 16, 64], F32, XR + 32768)
    angk = sb("angk", [128, 16, 64], F32, XR + 36864)
    angi = sb("angi", [128, 16, 64], I32, XR + 40960)
    UT = [sb("UT0", [128, 4, 272], F32, XR), sb("UT1", [128, 4, 272], F32, XR + 4352)]
    pa = sb("pa", [128, 4, 272], F32, XR + 8704)
    pbb = sb("pbb", [128, 4, 272], F32, XR + 13056)
    plT = [sb("plT0", [128, 2, 1024], BF16, XR + 17408), sb("plT1", [128, 2, 1024], BF16, XR + 21504)]
    invc = sb("invc", [128, 1024], F32, XR + 25600)
    x1 = sb("x1", [128, 8, 2048], F32, PH)
    gT = sb("gT", [128, 16, 1024], BF16, PH + 67584)
    h2T = sb("h2T", [128, 16, 1024], BF16, PH + 100352)
    fxn = [sb("fxn0", [128, 2048], F32, PH + 133120), sb("fxn1", [128, 2048], F32, PH + 141312)]
    junk2 = sb("junk2", [128, 2048], BF16, PH + 149504)
    sab = [sb("sa0", [128, 512], F32, PH + 133120), sb("sa1", [128, 512], F32, PH + 135168)]
    dtb = [sb("dt0", [128, 512], F32, PH + 137216), sb("dt1", [128, 512], F32, PH + 139264)]
    m4t = [sb("m4t0", [128, 512], F32, SC), sb("m4t1", [128, 512], F32, SC + 2048)]
    dgb = [sb("dg0", [128, 128], F32, SC + 4096), sb("dg1", [128, 128], F32, SC + 4608)]
    assert SC + 8032 <= avail + 1, (SC, avail)

    stack = ExitStack()
    with stack:
        banks = [stack.enter_context(nc.psum_tensor("bank%d" % i, [128, 512], F32)) for i in range(8)]
        S = Sched(nc, stack)
        for k in ("D_cst", "D_cstb", "D_ring0", "D_ring1", "D_ring2", "D_xb0", "D_xb1", "D_x1", "D_invc", "D_out", "D_dbg"):
            S.newsem(k)
        Rbank = [Res("bank%d" % i) for i in range(8)]
        bank_bf = [b[:, :].bitcast(BF16) for b in banks]
        Rring = [Res("ring0"), Res("ring1"), Res("ring2")]

        cident_f = cst[:, O_ID:O_ID + 128]
        ident_bf = cstb[:, OB_ID:OB_ID + 128]
        tri_bf = cstb[:, OB_TRI:OB_TRI + 512].rearrange("p (a q) -> p a q", a=2)
        eoh = cstb[:, OB_EOH:OB_EOH + 1024].rearrange("p (a m) -> p a m", a=8)

        R_cst = Res("cst")
        R_cstb = Res("cstb")
        S.op("sp", lambda e: e.dma_start(out=cst[:, :], in_=cst_d[:, :]), dma="D_cst")
        S.op("sp", lambda e: e.dma_start(out=posi[:, :], in_=pos_d[:, :]), dma="D_cst")
        R_cst.ws = {"D_cst": S.cnt["D_cst"]}
        S.op("pool", lambda e: e.dma_start(out=cstb[:, :], in_=cstb_d[:, :]), dma="D_cstb")
        S.op("pool", lambda e: e.dma_start(
            out=wpool[:, :, :, :], in_=w_pool.rearrange("g (cc p) d -> p g cc d", p=128)), dma="D_cstb")
        R_cstb.ws = {"D_cstb": S.cnt["D_cstb"]}

        R_ones = Res("ones")

        def f_ones(e):
            e.memset(ones_bf[:, :], 1.0)
            e.memset(ones_f[:, :], 1.0)
            return e.memset(one_f[:, :], 1.0)
        S.op("dve", f_ones, writes=[R_ones])

        def wv(ap):
            return ap.rearrange("(kc p) n -> p kc n", p=128)
        wsched = []

        def full(b):
            return b[:, :, :]
        for g in (1, 0):
            for j in range(4):
                wsched.append((("ada", g, j), [(full, wv(w_ada[:, g * D + j * 512:g * D + (j + 1) * 512]))]))

        def win(cg):
            return (("win", cg), [(full, wv(w_in[:, cg * 512:(cg + 1) * 512]))])
        wsched += [win(6), win(7)]
        ada_late = [(2, j) for j in range(4)] + [(3, j) for j in range(4)] + \
                   [(4, j) for j in range(4)] + [(5, j) for j in range(4)]
        for hg in range(2):
            wsched += [win(hg), win(2 + hg), win(4 + hg)]
            for (g, j) in ada_late[hg * 8:(hg + 1) * 8]:
                wsched.append((("ada", g, j), [(full, wv(w_ada[:, g * D + j * 512:g * D + (j + 1) * 512]))]))
        for cg in range(4):
            wsched.append((("wout", cg), [(full, wv(w_out[:, cg * 512:(cg + 1) * 512]))]))
        PHASES = [(0, 8), (8, 16), (16, 22)]
        for (u0, u1) in PHASES:
            for u in range(u0, u1):
                wsched.append((("gu", u), [
                    (lambda b: b[:, :, 0:256], wv(w_gate[:, u * 256:(u + 1) * 256])),
                    (lambda b: b[:, :, 256:512], wv(w_up[:, u * 256:(u + 1) * 256]))]))
            nfc = 2 * (u1 - u0)
            for dg in range(4):
                wsched.append((("wd", u0, dg), [
                    (lambda b, nfc=nfc: b[:, 0:nfc, :],
                     wv(w_down[u0 * 256:u1 * 256, dg * 512:(dg + 1) * 512]))]))
        wstate = {"issued": 0, "next": 0}

        def rbuf(n):
            return n % 3 if n < N_A3 else n % 2

        def w_issue():
            n = wstate["issued"]
            if n >= len(wsched):
                return
            b = rbuf(n)
            items = wsched[n][1]

            def f(e, items=items, b=b):
                return [e.dma_start(out=dst(ring[b]), in_=src) for (dst, src) in items]
            S.op("pool", f, writes=[Rring[b]], dma="D_ring%d" % b, ndma=len(items))
            wstate["issued"] = n + 1

        def ring_next(key):
            n = wstate["next"]
            assert wsched[n][0] == key, (wsched[n][0], key)
            while wstate["issued"] <= n:
                w_issue()
            w_issue_after = (wstate["issued"] == n + 1)
            wstate["next"] = n + 1
            return ring[rbuf(n)], Rring[rbuf(n)], w_issue_after

        def ring_prefetch():
            ahead = 2 if wstate["next"] < N_A3 - 1 else 1
            while wstate["issued"] < wstate["next"] + ahead:
                n0 = wstate["issued"]
                w_issue()
                if wstate["issued"] == n0:
                    break

        w_issue()
        w_issue()
        w_issue()

        def dump(name, ap_fn, shape, dt, reads):
            if name not in debug:
                return
            d = dbg_out(name, shape, dt)
            S.op("sp", lambda e: e.dma_start(out=d, in_=ap_fn()), reads=reads, dma="D_dbg")

        R_sf, R_sbf, R_row = Res("s_f"), Res("s_bf"), Res("rowst")
        R_mod = [Res("mod%d" % g) for g in range(6)]
        S.op("act", lambda e: e.activation(out=s_f[:, :], in_=cst[:, O_C:O_C + 16], func=AF.Silu),
             reads=[R_cst], writes=[R_sf])
        S.op("dve", lambda e: e.tensor_copy(out=s_bf[:, :], in_=s_f[:, :]), reads=[R_sf], writes=[R_sbf])

        def ada_chunk(g, j, b0, b1):
            buf, Rb, _ = ring_next(("ada", g, j))

            def f(e):
                for kc in range(16):
                    ins = e.matmul(banks[b0][0:1, :], lhsT=s_bf[:, kc:kc + 1], rhs=buf[:, kc, :],
                                   start=(kc == 0), stop=(kc == 15))
                return ins
            S.op("pe", f, reads=[Rb, R_sbf], writes=[Rbank[b0]])
            ring_prefetch()
            S.op("dve", lambda e: e.tensor_copy(out=rowst[0:1, :], in_=banks[b0][0:1, :]),
                 reads=[Rbank[b0]], writes=[R_row])

            def f2(e):
                for q in range(4):
                    ins = e.matmul(banks[b1][:, q:q + 1], lhsT=rowst[0:1, q * 128:(q + 1) * 128],
                                   rhs=one_f[0:1, 0:1], start=True, stop=True)
                return ins
            S.op("pe", f2, reads=[R_row, R_ones], writes=[Rbank[b1]])
            col = g * 16 + j * 4
            S.op("dve", lambda e: e.tensor_tensor(out=modT[:, col:col + 4], in0=banks[b1][:, 0:4],
                                                  in1=cst[:, O_BADA + col:O_BADA + col + 4], op=ALU.add),
                 reads=[Rbank[b1], R_cst], awrites=[R_mod[g]])

        for g in (1, 0):
            for j in range(4):
                ada_chunk(g, j, 0 + (j % 2), 2 + (j % 2))
        R_gm1, R_gm2 = Res("gm1"), Res("gm2")
        S.op("dve", lambda e: e.scalar_tensor_tensor(out=gm1T[:, :], in0=modT[:, 16:32], scalar=1.0,
                                                     in1=cst[:, O_GMIX:O_GMIX + 16], op0=ALU.add, op1=ALU.mult),
             reads=[R_mod[1], R_cst], writes=[R_gm1])
        dump("modT01", lambda: modT[:, 0:32], [128, 32], F32, [R_mod[0], R_mod[1]])

        R_rope = Res("rope")
        R_ang, R_angk, R_angi, R_posf = Res("ang"), Res("angk"), Res("angi"), Res("posf")
        A3 = lambda t: t[:, :, :]
        S.op("dve", lambda e: e.tensor_copy(out=posf[:, :], in_=posi[:, :]), reads=[R_cst], writes=[R_posf])
        S.op("dve", lambda e: e.tensor_tensor(
            out=A3(ang), in0=posf[:, :].unsqueeze(2).to_broadcast([128, 16, 64]),
            in1=cst[:, O_IFR:O_IFR + 64].unsqueeze(1).to_broadcast([128, 16, 64]), op=ALU.mult),
            reads=[R_posf, R_cst], writes=[R_ang])
        S.op("dve", lambda e: e.tensor_scalar(out=A3(angi), in0=A3(ang), scalar1=float(1.0 / TWO_PI), scalar2=None,
                                              op0=ALU.mult), reads=[R_ang], writes=[R_angi])
        S.op("dve", lambda e: e.tensor_copy(out=A3(angk), in_=A3(angi)), reads=[R_angi], writes=[R_angk])
        S.op("dve", lambda e: e.scalar_tensor_tensor(out=A3(ang), in0=A3(angk), scalar=-C1, in1=A3(ang),
                                                     op0=ALU.mult, op1=ALU.add), reads=[R_angk, R_ang], writes=[R_ang])
        S.op("dve", lambda e: e.scalar_tensor_tensor(out=A3(ang), in0=A3(angk), scalar=-C2, in1=A3(ang),
                                                     op0=ALU.mult, op1=ALU.add), reads=[R_angk, R_ang], writes=[R_ang])

        def wrap(lo_hi):
            for (thr, op_, add_) in lo_hi:
                S.op("dve", lambda e, thr=thr, op_=op_: e.tensor_single_scalar(out=A3(angk), in_=A3(ang), scalar=thr, op=op_),
                     reads=[R_ang], writes=[R_angk])
                S.op("dve", lambda e, add_=add_: e.scalar_tensor_tensor(out=A3(ang), in0=A3(angk), scalar=add_, in1=A3(ang),
                                                                        op0=ALU.mult, op1=ALU.add),
                     reads=[R_angk, R_ang], writes=[R_ang])
        wrap([(float(np.pi), ALU.is_gt, -TWO_PI), (float(-np.pi), ALU.is_lt, TWO_PI)])
        S.op("act", lambda e: e.activation(out=sinT[:, :, :], in_=A3(ang), func=AF.Sin),
             reads=[R_ang], awrites=[R_rope])
        S.op("dve", lambda e: e.tensor_scalar(out=A3(ang), in0=A3(ang), scalar1=float(np.pi / 2), scalar2=None, op0=ALU.add),
             reads=[R_ang], writes=[R_ang])
        wrap([(float(np.pi), ALU.is_gt, -TWO_PI)])
        S.op("act", lambda e: e.activation(out=cosT[:, :, :], in_=A3(ang), func=AF.Sin),
             reads=[R_ang], awrites=[R_rope])
        R_pb = Res("pbias")
        S.op("dve", lambda e: e.tensor_scalar(out=pbias[:, :], in0=cst[:, O_ALW:O_ALW + 32], scalar1=-NEG,
                                              scalar2=NEG, op0=ALU.mult, op1=ALU.add),
             reads=[R_cst], writes=[R_pb])
        S.barrier()

        R_xb = [Res("xb0"), Res("xb1")]
        R_xn = [Res("xn0"), Res("xn1")]
        R_hT = [Res("hT%d" % i) for i in range(17)]
        R_stc = [Res("st%d" % i) for i in range(32)]

        def norm_A(src_ap, np_, xbuf, Rx, xnbuf, Rxn, col, junk, Rjunk, x_is_sbuf=False):
            R_st = R_stc[col]
            if not x_is_sbuf:
                S.op("sp", lambda e: e.dma_start(out=xbuf[0:np_, :], in_=src_ap), writes=[Rx],
                     dma="D_" + Rx.name)
                xin = xbuf[0:np_, :]
            else:
                xin = src_ap
            S.op("act", lambda e: e.activation(out=junk[0:np_, :], in_=xin, func=AF.Square,
                                               accum_out=ssq[0:np_, col:col + 1]),
                 reads=[Rx], writes=[Rjunk, R_st])
            S.op("act", lambda e: e.activation(out=rsq[0:np_, col:col + 1], in_=ssq[0:np_, col:col + 1],
                                               func=AF.Sqrt, scale=1.0 / D, bias=EPS),
                 reads=[R_st], writes=[R_st])
            S.op("dve", lambda e: e.reciprocal(out=rstd[0:np_, col:col + 1], in_=rsq[0:np_, col:col + 1]),
                 reads=[R_st], writes=[R_st])
            S.op("act", lambda e: e.activation(out=xnbuf[0:np_, :], in_=xin, func=AF.Copy,
                                               scale=rstd[0:np_, col:col + 1]),
                 reads=[Rx, R_st], writes=[Rxn])

        def norm_B(np_, xnbuf, Rxn, gmT, Rgm, shT, Rsh, dst_fn, Rdst, bank0):
            for q4 in range(4):
                bk = bank0 + q4

                def ft(e, q4=q4, bk=bk):
                    for q in range(4):
                        kc = q4 * 4 + q
                        ins = e.transpose(banks[bk][:, q * 128:q * 128 + np_], xnbuf[0:np_, kc * 128:(kc + 1) * 128],
                                          cident_f[0:np_, 0:np_])
                    return ins
                S.op("pe", ft, reads=[Rxn, R_cst], writes=[Rbank[bk]])
                for q in range(4):
                    kc = q4 * 4 + q
                    S.op("dve", lambda e, q=q, kc=kc, bk=bk: e.tensor_scalar(
                        out=dst_fn(kc), in0=banks[bk][:, q * 128:q * 128 + np_],
                        scalar1=gmT[:, kc:kc + 1], scalar2=shT[:, kc:kc + 1], op0=ALU.mult, op1=ALU.add),
                        reads=[Rbank[bk], Rgm, Rsh], awrites=[Rdst])

        sh1T = modT[:, 0:16]
        R_junk = Res("junk")

        def m1_A(pos):
            i = m1_seq[pos]
            if i < 8:
                src = x_own[i * 128:(i + 1) * 128, :]
            elif i < 16:
                src = x_oth[(i - 8) * 128:(i - 7) * 128, :]
            else:
                src = x_halo[:, :]
            np_ = 128 if i < 16 else 64
            norm_A(src, np_, xb[pos % 2], R_xb[pos % 2], xnb[pos % 2], R_xn[pos % 2], i, junk1, R_junk)

        def m1_B(pos):
            i = m1_seq[pos]
            np_ = 128 if i < 16 else 64
            t0 = i * 128
            bank0 = 4 * (pos % 2) if pos < 9 else 4
            norm_B(np_, xnb[pos % 2], R_xn[pos % 2], gm1T, R_gm1, sh1T, R_mod[0],
                   lambda kc, t0=t0, np_=np_: hT[:, kc, t0:t0 + np_], R_hT[i], bank0)
        m1_seq = list(range(8)) + [16] + list(range(8, 16))

        R_UT = [Res("UT0"), Res("UT1")]
        R_pa, R_pb2 = Res("pa"), Res("pbb")
        R_pl = [Res("pl0"), Res("pl1")]
        R_invc = Res("invc")
        R_cT = [Res("cT%d" % s_) for s_ in range(4)]
        hv_b = cst[:, O_HV:O_HV + 64].rearrange("p (s t) -> p s t", s=4)
        poolA = [0, 1, 2, 3]
        pa_i = [0]

        def nextbank(pool_, ctr):
            b = pool_[ctr[0] % len(pool_)]
            ctr[0] += 1
            return b
        pend_pool = []

        def pool_mm(g, pl):
            for dc in range(2):
                for th in range(2):
                    bk = nextbank(poolA, pa_i)

                    def fp(e, bk=bk, dc=dc, th=th):
                        for c2 in range(2):
                            ins = e.matmul(banks[bk][:, :], lhsT=wpool[:, g, c2, dc * 128:(dc + 1) * 128],
                                           rhs=pl[:, c2, th * 512:(th + 1) * 512], start=(c2 == 0), stop=(c2 == 1))
                        return ins
                    S.op("pe", fp, reads=[R_pl[g % 2], R_cstb], writes=[Rbank[bk]])
                    ch = g * 2 + dc
                    S.op("act", lambda e, bk=bk, ch=ch, th=th: e.activation(
                        out=concatT[:, 8 + ch, th * 512:(th + 1) * 512], in_=banks[bk][:, :], func=AF.Copy,
                        scale=cst[:, O_PSC + ch:O_PSC + ch + 1]),
                        reads=[Rbank[bk], R_cst], awrites=[R_cT[2 * th], R_cT[2 * th + 1]])

        t2s = {"buf": None, "Rb": None}

        def t2_chunk(c8):
            if True:
                cgu, cc = c8 // 4, c8 % 4
                first = (cc == 0)
                if first:
                    t2s["buf"], t2s["Rb"], _ = ring_next(("win", 6 + cgu))
                buf, Rb = t2s["buf"], t2s["Rb"]
                g = c8 // 2
                w = POOL_WINDOWS[g]
                ub = c8 % 2
                if c8 % 2 == 0:
                    S.op("sp", lambda e, g=g: e.dma_start(out=invc[:, :], in_=invcnt_d[g]), writes=[R_invc],
                         dma="D_invc")
                for th in range(2):
                    bk = nextbank(poolA, pa_i)

                    def f(e, bk=bk, cc=cc, th=th, buf=buf):
                        for kc in range(16):
                            ins = e.matmul(banks[bk][:, :], lhsT=buf[:, kc, cc * 128:(cc + 1) * 128],
                                           rhs=hT[:, kc, th * 512:(th + 1) * 512], start=(kc == 0), stop=(kc == 15))
                        return ins
                    S.op("pe", f, reads=[Rb] + R_hT[4 * th:4 * th + 4], writes=[Rbank[bk]])
                    if first:
                        ring_prefetch()
                        first = False
                    S.op("act", lambda e, bk=bk, ub=ub, th=th: e.activation(
                        out=UT[ub][:, 2 * th:2 * th + 2, 16:272],
                        in_=banks[bk][:, :].rearrange("p (s t) -> p s t", s=2), func=AF.Copy),
                        reads=[Rbank[bk]], awrites=[R_UT[ub]])
                bk = nextbank(poolA, pa_i)

                def fh(e, bk=bk, cc=cc, buf=buf):
                    for kc in range(16):
                        ins = e.matmul(banks[bk][:, 0:64], lhsT=buf[:, kc, cc * 128:(cc + 1) * 128],
                                       rhs=hT[:, kc, 2048:2112], start=(kc == 0), stop=(kc == 15))
                    return ins
                S.op("pe", fh, reads=[Rb, R_hT[16]], writes=[Rbank[bk]])
                S.op("dve", lambda e, bk=bk, ub=ub: e.tensor_tensor(
                    out=UT[ub][:, :, 0:16], in0=banks[bk][:, 0:64].rearrange("p (s t) -> p s t", s=4),
                    in1=hv_b, op=ALU.mult), reads=[Rbank[bk], R_cst], awrites=[R_UT[ub]])
                src_t, Rsrc = UT[ub], R_UT[ub]
                step = 1
                vs = 0
                while step < w:
                    dst_t, Rd = (pa, R_pa) if src_t is not pa else (pbb, R_pb2)
                    S.op("dve", lambda e, s_=src_t, d_=dst_t, step=step, vs=vs: e.tensor_tensor(
                        out=d_[:, :, vs + step:272], in0=s_[:, :, vs + step:272], in1=s_[:, :, vs:272 - step], op=ALU.add),
                        reads=[Rsrc], writes=[Rd])
                    src_t, Rsrc = dst_t, Rd
                    vs += step
                    step *= 2
                tmp_t, Rt = (pa, R_pa) if src_t is not pa else (pbb, R_pb2)
                S.op("dve", lambda e, s_=src_t, t_=tmp_t: e.tensor_tensor(
                    out=t_[:, :, 16:272], in0=s_[:, :, 16:272],
                    in1=invc[:, :].rearrange("p (s t) -> p s t", s=4), op=ALU.mult),
                    reads=[Rsrc, R_invc], writes=[Rt])
                pl = plT[g % 2]
                S.op("dve", lambda e, t_=tmp_t, ub=ub, pl=pl, c8=c8: e.tensor_tensor(
                    out=pl[:, c8 % 2, :].rearrange("p (s t) -> p s t", s=4), in0=t_[:, :, 16:272],
                    in1=UT[ub][:, :, 16:272], op=ALU.subtract),
                    reads=[Rt, R_UT[ub]], awrites=[R_pl[g % 2]])
                if pend_pool:
                    pool_mm(*pend_pool.pop(0))
                if c8 % 2 == 1:
                    pend_pool.append((g, pl))

        m1_A(0)
        for pos in range(9):
            m1_A(pos + 1)
            m1_B(pos)
        for c8 in range(8):
            pos = 9 + c8
            if pos + 1 < 17:
                m1_A(pos + 1)
            t2_chunk(c8)
            m1_B(pos)
        while pend_pool:
            pool_mm(*pend_pool.pop(0))
        dump("hT", lambda: hT[:, :, :], [128, 16, 2112], BF16, R_hT)
        dump("poolT", lambda: concatT[:, 8:16, :], [128, 8, 1024], BF16, R_cT)
        S.barrier()

        R_QT = [Res("QT%d" % i) for i in range(8)]
        R_KT = [Res("KT%d" % i) for i in range(16)]
        R_V = [Res("V%d" % i) for i in range(16)]
        R_km = Res("kmean")
        poolP = [0, 1, 2, 5, 6]
        poolTr = [3, 4]
        pp_i, pt_i = [0], [0]
        gq_b = cst[:, O_GQ:O_GQ + 128]
        gk_b = cst[:, O_GK:O_GK + 128]

        def v4(t):
            return t.rearrange("p (h d) -> p h d", h=4)

        def v5(t):
            return t.rearrange("p (h a d) -> p h a d", h=4, a=2)

        SCR = [dict(jq=jq[0], R_jq=Res("jq0"), sq=sq, qn=qn, ta=ta, qrb=qrb, st=st4, R_sq=Res("sq"), R_qn=Res("qn"), R_ta=Res("ta"),
                    R_qrb=Res("qrb"), R_s4=Res("st4")),
               dict(jq=jq[1], R_jq=Res("jq1"), sq=sq2, qn=qn2, ta=ta2, qrb=qrb2, st=st4b, R_sq=Res("sq2"), R_qn=Res("qn2"), R_ta=Res("ta2"),
                    R_qrb=Res("qrb2"), R_s4=Res("st4b"))]

        def qk_part1(bk, i, g_b, sc):
            sq_, qn_, ta_, qrb_, st_ = sc["sq"], sc["qn"], sc["ta"], sc["qrb"], sc["st"]
            R_sq_, R_qn_, R_ta_, R_qrb_, R_s4_ = sc["R_sq"], sc["R_qn"], sc["R_ta"], sc["R_qrb"], sc["R_s4"]

            jq_ = sc["jq"]

            def fsq(e):
                for h in range(4):
                    ins = e.activation(out=jq_[:, h * 128:(h + 1) * 128], in_=banks[bk][:, h * 128:(h + 1) * 128],
                                       func=AF.Square, accum_out=st_[:, h:h + 1])
                return ins
            S.op("act", fsq, reads=[Rbank[bk]], writes=[sc["R_jq"], R_s4_])
            S.op("act", lambda e: e.activation(out=st_[:, 4:8], in_=st_[:, 0:4], func=AF.Sqrt, scale=1.0 / DH, bias=EPS),
                 reads=[R_s4_], writes=[R_s4_])
            S.op("dve", lambda e: e.reciprocal(out=st_[:, 8:12], in_=st_[:, 4:8]), reads=[R_s4_], writes=[R_s4_])

        def qk_part1b(bk, i, g_b, sc):
            sq_, qn_, ta_, qrb_, st_ = sc["sq"], sc["qn"], sc["ta"], sc["qrb"], sc["st"]
            R_sq_, R_qn_, R_ta_, R_qrb_, R_s4_ = sc["R_sq"], sc["R_qn"], sc["R_ta"], sc["R_qrb"], sc["R_s4"]

            def fqn(e):
                for h in range(4):
                    ins = e.activation(out=qn_[:, h * 128:(h + 1) * 128], in_=banks[bk][:, h * 128:(h + 1) * 128],
                                       func=AF.Copy, scale=st_[:, 8 + h:9 + h])
                return ins
            S.op("act", fqn, reads=[Rbank[bk], R_s4_], writes=[R_qn_])
            S.op("dve", lambda e: e.tensor_tensor(
                out=v4(qn_[:, :]), in0=v4(qn_[:, :]), in1=g_b.unsqueeze(1).to_broadcast([128, 4, 128]), op=ALU.mult),
                reads=[R_qn_, R_cst], writes=[R_qn_])
            cs_b = cosT[:, i, :].unsqueeze(1).unsqueeze(1).to_broadcast([128, 4, 2, 64])
            sn_b = sinT[:, i, :].unsqueeze(1).to_broadcast([128, 4, 64])
            S.op("dve", lambda e: e.tensor_tensor(out=v5(ta_[:, :]), in0=v5(qn_[:, :]), in1=cs_b, op=ALU.mult),
                 reads=[R_qn_, R_rope], writes=[R_ta_])

            def fb(e):
                e.tensor_tensor(out=v5(sq_[:, :])[:, :, 0, :], in0=v5(qn_[:, :])[:, :, 1, :], in1=sn_b, op=ALU.mult)
                return e.tensor_tensor(out=v5(sq_[:, :])[:, :, 1, :], in0=v5(qn_[:, :])[:, :, 0, :], in1=sn_b, op=ALU.mult)
            S.op("dve", fb, reads=[R_qn_, R_rope], writes=[R_sq_])

            def fo(e):
                e.tensor_tensor(out=v5(qrb_[:, :])[:, :, 0, :], in0=v5(ta_[:, :])[:, :, 0, :],
                                in1=v5(sq_[:, :])[:, :, 0, :], op=ALU.subtract)
                return e.tensor_tensor(out=v5(qrb_[:, :])[:, :, 1, :], in0=v5(ta_[:, :])[:, :, 1, :],
                                       in1=v5(sq_[:, :])[:, :, 1, :], op=ALU.add)
            S.op("dve", fo, reads=[R_ta_, R_sq_], writes=[R_qrb_])

        def qk_part2(i, sc, dstT, Rdst):
            qrb_, R_qrb_ = sc["qrb"], sc["R_qrb"]
            tb = nextbank(poolTr, pt_i)

            def ftr(e):
                for h in range(4):
                    ins = e.transpose(bank_bf[tb][:, h * 128:(h + 1) * 128], qrb_[:, h * 128:(h + 1) * 128], ident_bf)
                return ins
            S.op("pe", ftr, reads=[R_qrb_, R_cstb], writes=[Rbank[tb]])
            S.op("act", lambda e: e.activation(out=dstT[:, :, i * 128:(i + 1) * 128],
                                               in_=v4(bank_bf[tb][:, 0:512]), func=AF.Copy),
                 reads=[Rbank[tb]], writes=[Rdst])

        poolS = [0, 1, 2]
        ps_i = [0]
        R_bT = [Res("biasT0"), Res("biasT1")]
        R_PT = [Res("PT0"), Res("PT1"), Res("PT2")]
        PT.append(PT2)
        R_gw, R_t8, R_sel, R_rden = Res("gmw"), Res("top8"), Res("selw"), Res("rden")
        alw4 = cst[:, O_ALW:O_ALW + 32].rearrange("p (s k) -> p s k", s=4)
        diag4 = cst[:, O_DIAG:O_DIAG + 32].rearrange("p (s k) -> p s k", s=4)
        pb4 = pbias[:, :].rearrange("p (s k) -> p s k", s=4)

        def v428(t):
            return t.rearrange("p (s a k) -> p s a k", s=4, a=2)

        OB, DBK = (3, 5), (4, 6)

        def attn_core(hl, hb, h):
            tiles = []
            for j in range(4):
                q0 = j * 256
                chunks = []
                if q0 < 512:
                    chunks.append((q0, 512))
                chunks.append((max(q0, 512), 1024))
                for typ in range(2):
                    kb = j + 4 * typ
                    for (qa, qb) in chunks:
                        for kt in range(2):
                            tiles.append((j, typ, kb, qa, qb, kt))
            ntile = len(tiles)
            first_seen = {}
            last_seen = {}
            for t_i, (j, typ, kb, qa, qb, kt) in enumerate(tiles):
                bsel = 0 if qa < 512 else 1
                first_seen.setdefault(bsel, t_i)
                last_seen[bsel] = t_i

            def emit_s(t_i):
                j, typ, kb, qa, qb, kt = tiles[t_i]
                n = qb - qa
                sbk = nextbank(poolS, ps_i)
                ktile = kb * 2 + kt
                has_diag = (typ == 0) and (qa <= j * 256 < qb)

                def fs(e):
                    o_ = banks[sbk][:, 0:n]
                    e.matmul(o_, lhsT=KT[:, hl, ktile * 128:(ktile + 1) * 128], rhs=QT[:, hl, qa:qb], start=True, stop=False)
                    ins = e.matmul(o_, lhsT=eoh[0:32, kb, :], rhs=biasT[hb][0:32, qa:qb], start=False, stop=(not has_diag))
                    if has_diag:
                        d0 = j * 256 - qa
                        ins = e.matmul(banks[sbk][:, d0:d0 + 256], lhsT=ident_bf, rhs=tri_bf[:, kt, :], start=False, stop=True)
                    return ins
                S.op("pe", fs, reads=[R_KT[ktile], R_bT[hb], R_cstb] + R_QT[qa // 128:qb // 128], writes=[Rbank[sbk]])
                pb_i = t_i % 3
                S.op("act", lambda e: e.activation(out=PT[pb_i][:, 0:n], in_=banks[sbk][:, 0:n], func=AF.Exp,
                                                   scale=float(DH ** -0.5)),
                     reads=[Rbank[sbk]], writes=[R_PT[pb_i]])
                return (t_i, pb_i)

            def emit_pv(t_i, pb_i):
                j, typ, kb, qa, qb, kt = tiles[t_i]
                n = qb - qa
                bsel = 0 if qa < 512 else 1
                ob, db = OB[bsel], DBK[bsel]
                c0 = qa - 512 * bsel
                ktile = kb * 2 + kt
                first = (first_seen[bsel] == t_i)
                last = (last_seen[bsel] == t_i)

                def fpv(e):
                    e.matmul(banks[ob][:, c0:c0 + n], lhsT=Vt[:, ktile, hl * 128:(hl + 1) * 128], rhs=PT[pb_i][:, 0:n],
                             start=first, stop=last)
                    return e.matmul(banks[db][:, c0:c0 + n], lhsT=ones_bf[:, :], rhs=PT[pb_i][:, 0:n], start=first, stop=last)
                wr = [Rbank[ob], Rbank[db]] if first else []
                aw = [] if first else [Rbank[ob], Rbank[db]]
                S.op("pe", fpv, reads=[R_PT[pb_i], R_V[ktile], R_ones], writes=wr, awrites=aw)
            pend = []
            for t_i in range(ntile):
                cur = emit_s(t_i)
                if len(pend) >= 2:
                    emit_pv(*pend.pop(0))
                pend.append(cur)
            while pend:
                emit_pv(*pend.pop(0))
            for bsel in range(2):
                ob, db = OB[bsel], DBK[bsel]
                S.op("dve", lambda e, db=db: e.reciprocal(out=rden[:, :], in_=banks[db][:, :]),
                     reads=[Rbank[db]], writes=[R_rden])
                S.op("dve", lambda e, ob=ob, bsel=bsel: e.tensor_tensor(
                    out=concatT[:, h, bsel * 512:(bsel + 1) * 512], in0=banks[ob][:, :], in1=rden[:, :], op=ALU.mult),
                    reads=[Rbank[ob], R_rden], awrites=[R_cT[2 * bsel], R_cT[2 * bsel + 1]])

        def gate_sel(hl):

            def fg(e):
                for qt in range(8):
                    ins = e.matmul(banks[7][:, qt * 8:(qt + 1) * 8], lhsT=QT[:, hl, qt * 128:(qt + 1) * 128],
                                   rhs=kmean_bf[:, hl * 8:(hl + 1) * 8], start=True, stop=True)
                return ins
            S.op("pe", fg, reads=R_QT + [R_km], writes=[Rbank[7]])
            S.op("dve", lambda e: e.tensor_tensor(
                out=v428(gmw[:, :]), in0=v428(banks[7][:, 0:64]),
                in1=pb4.unsqueeze(2).to_broadcast([128, 4, 2, 8]), op=ALU.add),
                reads=[Rbank[7], R_pb], writes=[R_gw])

            def fmax(e):
                for qt in range(8):
                    ins = e.max(out=top8[:, qt * 8:(qt + 1) * 8], in_=gmw[:, qt * 8:(qt + 1) * 8])
                return ins
            S.op("dve", fmax, reads=[R_gw], writes=[R_t8])

            S.op("dve", lambda e: e.tensor_tensor(
                out=selw[:, :].rearrange("p (q k) -> p q k", q=8), in0=gmw[:, :].rearrange("p (q k) -> p q k", q=8),
                in1=top8[:, :].rearrange("p (q k) -> p q k", q=8)[:, :, 2:3].to_broadcast([128, 8, 8]), op=ALU.is_ge),
                reads=[R_gw, R_t8], writes=[R_sel])
            S.op("dve", lambda e: e.tensor_tensor(out=v428(selw[:, :]), in0=v428(selw[:, :]),
                                                  in1=alw4.unsqueeze(2).to_broadcast([128, 4, 2, 8]), op=ALU.mult),
                 reads=[R_sel, R_cst], writes=[R_sel])
            S.op("dve", lambda e: e.tensor_tensor(out=v428(selw[:, :]), in0=v428(selw[:, :]),
                                                  in1=diag4.unsqueeze(2).to_broadcast([128, 4, 2, 8]), op=ALU.add),
                 reads=[R_sel, R_cst], writes=[R_sel])
            S.op("dve", lambda e: e.tensor_scalar(out=selw[:, :], in0=selw[:, :], scalar1=-NEG, scalar2=NEG,
                                                  op0=ALU.mult, op1=ALU.add), reads=[R_sel], writes=[R_sel])

        def bias_T(hl):
            hb = hl % 2

            def bias_half(half):
                def ftb(e):
                    for q in range(4):
                        qt = half * 4 + q
                        ins = e.transpose(banks[7][0:8, q * 128:(q + 1) * 128], selw[:, qt * 8:(qt + 1) * 8], cident_f)
                    return ins
                S.op("pe", ftb, reads=[R_sel, R_cst], writes=[Rbank[7]])
                S.op("act", lambda e: e.activation(out=biasT[hb][0:8, half * 512:(half + 1) * 512],
                                                   in_=banks[7][0:8, :], func=AF.Copy),
                     reads=[Rbank[7]], writes=[R_bT[hb]] if half == 0 else [], awrites=[] if half == 0 else [R_bT[hb]])
            bias_half(0)
            bias_half(1)

        def attention(hg, ada_list):
            for hb_ in range(2):
                S.op("dve", lambda e, hb_=hb_: e.memset(biasT[hb_][0:32, :], 0.0), writes=[R_bT[hb_]])
            gate_sel(0)
            bias_T(0)
            for hl in range(4):
                if hl + 1 < 4:
                    gate_sel(hl + 1)
                attn_core(hl, hl % 2, hg * 4 + hl)
                if hl + 1 < 4:
                    bias_T(hl + 1)
                for (g, j) in ada_list[hl * 2:hl * 2 + 2]:
                    ada_chunk(g, j, 7, 7)

        for hg in range(2):
            tl = [("q", i, hg) for i in range(8)] + [("k", i, 2 + hg) for i in range(16)] + \
                 [("v", i, 4 + hg) for i in range(16)]
            info = {}
            cur = {"cg": None, "buf": None, "Rb": None}

            def stage_M(t):
                which, i, cg = tl[t]
                if cur["cg"] != cg:
                    cur["buf"], cur["Rb"], _ = ring_next(("win", cg))
                    cur["cg"] = cg
                    newc = True
                else:
                    newc = False
                buf, Rb = cur["buf"], cur["Rb"]
                bk = nextbank(poolP, pp_i)

                def f(e):
                    for kc in range(16):
                        ins = e.matmul(banks[bk][:, :], lhsT=hT[:, kc, i * 128:(i + 1) * 128], rhs=buf[:, kc, :],
                                       start=(kc == 0), stop=(kc == 15))
                    return ins
                S.op("pe", f, reads=[Rb, R_hT[i]], writes=[Rbank[bk]])
                if newc:
                    ring_prefetch()
                sc = None
                if which != "v":
                    sc = SCR[tcount[0] % 2]
                    tcount[0] += 1
                info[t] = (bk, sc)

            def stage_A(t):
                which, i, cg = tl[t]
                bk, sc = info[t]
                if which == "v":
                    S.op("act", lambda e: e.activation(out=Vt[:, i, :], in_=banks[bk][:, :], func=AF.Copy),
                         reads=[Rbank[bk]], writes=[R_V[i]])
                else:
                    qk_part1(bk, i, None, sc)

            def stage_B(t):
                which, i, cg = tl[t]
                bk, sc = info[t]
                if which != "v":
                    qk_part1b(bk, i, gq_b if which == "q" else gk_b, sc)

            def stage_C(t):
                which, i, cg = tl[t]
                bk, sc = info[t]
                if which == "q":
                    qk_part2(i, sc, QT, R_QT[i])
                elif which == "k":
                    qk_part2(i, sc, KT, R_KT[i])
            tcount = [0]
            nt = len(tl)
            for t in range(nt + 3):
                if t < nt:
                    stage_M(t)
                if 0 <= t - 1 < nt:
                    stage_A(t - 1)
                if 0 <= t - 2 < nt:
                    stage_B(t - 2)
                if 0 <= t - 3 < nt:
                    stage_C(t - 3)
            S.op("dve", lambda e: e.tensor_reduce(
                out=km_f[:, :], in_=KT[:, :, :].rearrange("p h (k t) -> p (h k) t", k=8), axis=AX.X, op=ALU.add),
                reads=R_KT, writes=[R_km])
            S.op("dve", lambda e: e.tensor_scalar(out=kmean_bf[:, :], in0=km_f[:, :], scalar1=1.0 / 256.0, scalar2=None,
                                                  op0=ALU.mult), reads=[R_km], writes=[R_km])
            if hg == 0:
                dump("QT0", lambda: QT[:, :, :], [128, 4, 1024], BF16, R_QT)
                dump("KT0", lambda: KT[:, :, :], [128, 4, 2048], BF16, R_KT)
                dump("V0", lambda: Vt[:, :, :], [128, 16, 512], BF16, R_V)
            S.barrier()
            if hg == 1:
                R_x1 = [Res("x1_%d" % i) for i in range(8)]
                for i in range(8):
                    S.op("sp", lambda e, i=i: e.dma_start(out=x1[:, i, :], in_=x_own[i * 128:(i + 1) * 128, :]),
                         dma="D_x1")
                for i in range(8):
                    R_x1[i].ws = {"D_x1": S.cnt["D_x1"]}
            attention(hg, ada_late[hg * 8:(hg + 1) * 8])
            if hg == 0:
                dump("att0", lambda: concatT[:, 0:4, :], [128, 4, 1024], BF16, R_cT)
            S.barrier()
        dump("concatT", lambda: concatT[:, :, :], [128, 16, 1024], BF16, R_cT)
        dump("modT", lambda: modT[:, :], [128, 96], F32, R_mod)

        R_bc = Res("bc")
        R_dg = [Res("dg0"), Res("dg1")]

        def build_bc(g):
            first = True
            for c4 in range(4):
                bk = 3 + (c4 % 2)
                for q in range(4):
                    c = c4 * 4 + q
                    d_ = c % 2
                    S.op("dve", lambda e, c=c, d_=d_: e.tensor_scalar(
                        out=dgb[d_][:, :], in0=cident_f, scalar1=modT[:, g * 16 + c:g * 16 + c + 1], scalar2=None,
                        op0=ALU.mult), reads=[R_cst, R_mod[g]], writes=[R_dg[d_]])
                    S.op("pe", lambda e, bk=bk, q=q, d_=d_: e.matmul(
                        banks[bk][:, q * 128:(q + 1) * 128], lhsT=ones_f[:, :], rhs=dgb[d_][:, :], start=True, stop=True),
                        reads=[R_dg[d_], R_ones], writes=[Rbank[bk]] if q == 0 else [], awrites=[] if q == 0 else [Rbank[bk]])
                if first:
                    S.op("act", lambda e, bk=bk, c4=c4: e.activation(out=bc[:, c4 * 512:(c4 + 1) * 512], in_=banks[bk][:, :],
                                                                     func=AF.Copy), reads=[Rbank[bk]], writes=[R_bc])
                    first = False
                else:
                    S.op("act", lambda e, bk=bk, c4=c4: e.activation(out=bc[:, c4 * 512:(c4 + 1) * 512], in_=banks[bk][:, :],
                                                                     func=AF.Copy), reads=[Rbank[bk]], awrites=[R_bc])

        build_bc(2)
        R_m4t = [Res("m4t0"), Res("m4t1")]
        poolO = [0, 1, 2, 5, 6, 7]
        po_i = [0]
        mt_i = 0
        for cg in range(4):
            buf, Rb, _ = ring_next(("wout", cg))
            for i in range(8):
                bk = nextbank(poolO, po_i)

                def f(e, bk=bk, i=i, buf=buf):
                    for kc in range(16):
                        ins = e.matmul(banks[bk][:, :], lhsT=concatT[:, kc, i * 128:(i + 1) * 128], rhs=buf[:, kc, :],
                                       start=(kc == 0), stop=(kc == 15))
                    return ins
                S.op("pe", f, reads=[Rb, R_cT[i // 2]], writes=[Rbank[bk]])
                if i == 0:
                    ring_prefetch()
                mt = mt_i % 2
                mt_i += 1
                S.op("dve", lambda e, bk=bk, mt=mt, cg=cg: e.tensor_tensor(
                    out=m4t[mt][:, :], in0=banks[bk][:, :], in1=bc[:, cg * 512:(cg + 1) * 512], op=ALU.mult),
                    reads=[Rbank[bk], R_bc], writes=[R_m4t[mt]])
                S.op("dve", lambda e, mt=mt, i=i, cg=cg: e.tensor_tensor(
                    out=x1[:, i, cg * 512:(cg + 1) * 512], in0=x1[:, i, cg * 512:(cg + 1) * 512], in1=m4t[mt][:, :], op=ALU.add),
                    reads=[R_m4t[mt], R_x1[i]], awrites=[R_x1[i]])
        dump("x1", lambda: x1[:, :, :], [128, 8, 2048], F32, R_x1)
        S.barrier()

        S.op("dve", lambda e: e.scalar_tensor_tensor(out=gm2T[:, :], in0=modT[:, 64:80], scalar=1.0,
                                                     in1=cst[:, O_GFFN:O_GFFN + 16], op0=ALU.add, op1=ALU.mult),
             reads=[R_mod[4], R_cst], writes=[R_gm2])
        R_fxn = [Res("fxn0"), Res("fxn1")]
        R_h2 = [Res("h2T%d" % i) for i in range(8)]
        sh2T = modT[:, 48:64]
        R_junk2 = Res("junk2")

        def f1_A(i):
            norm_A(x1[:, i, :], 128, None, R_x1[i], fxn[i % 2], R_fxn[i % 2], 17 + i, junk2, R_junk2, x_is_sbuf=True)

        def f1_B(i):
            norm_B(128, fxn[i % 2], R_fxn[i % 2], gm2T, R_gm2, sh2T, R_mod[3],
                   lambda kc, i=i: h2T[:, kc, i * 128:(i + 1) * 128], R_h2[i], 4 * (i % 2))
        f1_A(0)
        for i in range(8):
            if i + 1 < 8:
                f1_A(i + 1)
            f1_B(i)
        dump("h2T", lambda: h2T[:, :, :], [128, 16, 1024], BF16, R_h2)
        build_bc(5)
        S.barrier()

        R_gT = [[Res("gT%d_%d" % (fc, th)) for th in range(2)] for fc in range(16)]
        R_sa = [Res("sa0"), Res("sa1")]
        R_dt = [Res("dt0"), Res("dt1")]
        poolGa, poolGb, poolD = [0, 1], [2, 3], [4, 5, 6, 7]
        ga_i, gb_i, pd_i = [0], [0], [0]
        sa_i = 0
        dt_i = 0
        nout = 0
        for pi, (u0, u1) in enumerate(PHASES):
            nfc = 2 * (u1 - u0)
            for u in range(u0, u1):
                buf, Rb, _ = ring_next(("gu", u))
                for fcl in range(2):
                    fc = 2 * (u - u0) + fcl
                    for th in range(2):
                        ba = nextbank(poolGa, ga_i)
                        bb = nextbank(poolGb, gb_i)

                        def fa(e, ba=ba, fcl=fcl, th=th, buf=buf):
                            for kc in range(16):
                                ins = e.matmul(banks[ba][:, :], lhsT=buf[:, kc, fcl * 128:(fcl + 1) * 128],
                                               rhs=h2T[:, kc, th * 512:(th + 1) * 512], start=(kc == 0), stop=(kc == 15))
                            return ins

                        def fb_(e, bb=bb, fcl=fcl, th=th, buf=buf):
                            for kc in range(16):
                                ins = e.matmul(banks[bb][:, :], lhsT=buf[:, kc, 256 + fcl * 128:256 + (fcl + 1) * 128],
                                               rhs=h2T[:, kc, th * 512:(th + 1) * 512], start=(kc == 0), stop=(kc == 15))
                            return ins
                        S.op("pe", fa, reads=[Rb] + R_h2[4 * th:4 * th + 4], writes=[Rbank[ba]])
                        S.op("pe", fb_, reads=[Rb] + R_h2[4 * th:4 * th + 4], writes=[Rbank[bb]])
                        if fcl == 0 and th == 0:
                            ring_prefetch()
                        si = sa_i % 2
                        sa_i += 1
                        S.op("act", lambda e, ba=ba, si=si: e.activation(out=sab[si][:, :], in_=banks[ba][:, :], func=AF.Silu),
                             reads=[Rbank[ba]], writes=[R_sa[si]])
                        S.op("dve", lambda e, bb=bb, si=si, fc=fc, th=th: e.tensor_tensor(
                            out=gT[:, fc, th * 512:(th + 1) * 512], in0=sab[si][:, :], in1=banks[bb][:, :], op=ALU.mult),
                            reads=[R_sa[si], Rbank[bb]], writes=[R_gT[fc][th]])
            for dg in range(4):
                buf, Rb, _ = ring_next(("wd", u0, dg))
                for i in range(8):
                    bk = nextbank(poolD, pd_i)

                    def fd(e, bk=bk, i=i, buf=buf, nfc=nfc):
                        for fc in range(nfc):
                            ins = e.matmul(banks[bk][:, :], lhsT=gT[:, fc, i * 128:(i + 1) * 128], rhs=buf[:, fc, :],
                                           start=(fc == 0), stop=(fc == nfc - 1))
                        return ins
                    S.op("pe", fd, reads=[Rb] + [R_gT[fc][i // 4] for fc in range(nfc)], writes=[Rbank[bk]])
                    if i == 0:
                        ring_prefetch()
                    di = dt_i % 2
                    dt_i += 1
                    S.op("dve", lambda e, bk=bk, di=di, dg=dg: e.tensor_tensor(
                        out=dtb[di][:, :], in0=banks[bk][:, :], in1=bc[:, dg * 512:(dg + 1) * 512], op=ALU.mult),
                        reads=[Rbank[bk], R_bc], writes=[R_dt[di]])
                    S.op("dve", lambda e, di=di, i=i, dg=dg: e.tensor_tensor(
                        out=x1[:, i, dg * 512:(dg + 1) * 512], in0=x1[:, i, dg * 512:(dg + 1) * 512], in1=dtb[di][:, :],
                        op=ALU.add), reads=[R_dt[di], R_x1[i]], awrites=[R_x1[i]])
                    if pi == len(PHASES) - 1 and dg == 3:
                        S.op("sp", lambda e, i=i: e.dma_start(out=out_d[i * 128:(i + 1) * 128, :], in_=x1[:, i, :]),
                             reads=[R_x1[i]], dma="D_out")
                        nout += 1
        S.final_wait("sp", "D_out")
        if S.cnt["D_dbg"]:
            S.final_wait("sp", "D_dbg")

        block = stack.enter_context(nc.Block())

        @block.tensor
        def _(e):
            for r in S.prog["pe"]:
                r(e)

        @block.scalar
        def _(e):
            for r in S.prog["act"]:
                r(e)

        @block.vector
        def _(e):
            for r in S.prog["dve"]:
                r(e)

        @block.gpsimd
        def _(e):
            for r in S.prog["pool"]:
                r(e)

        @block.sync
        def _(e):
            for r in S.prog["sp"]:
                r(e)
    return nc, dbg


def host_inputs(x, c, positions, w_ada, b_ada, g_mix_norm, w_in, g_q, g_k, w_pool, pool_scale, w_out,
                g_ffn_norm, w_gate, w_up, w_down):
    f32 = np.float32
    x = np.asarray(x, f32)
    c = np.asarray(c, f32)
    positions = np.asarray(positions, np.int32)
    shared = {
        "w_ada": np.ascontiguousarray(np.asarray(w_ada, f32)[0]),
        "w_in": np.ascontiguousarray(np.asarray(w_in, f32)[0]),
        "w_pool": np.ascontiguousarray(np.asarray(w_pool, f32)[0]),
        "w_out": np.ascontiguousarray(np.asarray(w_out, f32)[0]),
        "w_gate": np.ascontiguousarray(np.asarray(w_gate, f32)[0]),
        "w_up": np.ascontiguousarray(np.asarray(w_up, f32)[0]),
        "w_down": np.ascontiguousarray(np.asarray(w_down, f32)[0]),
    }
    b_ada = np.asarray(b_ada, f32)[0]
    gmix = np.asarray(g_mix_norm, f32)[0]
    gffn = np.asarray(g_ffn_norm, f32)[0]
    gq = np.asarray(g_q, f32)[0]
    gk = np.asarray(g_k, f32)[0]
    psc = np.asarray(pool_scale, f32)[0]
    inv_freq = (10000.0 ** (-np.arange(0, DH, 2, dtype=f32) / f32(DH))).astype(f32)
    in_maps = []
    for core in range(8):
        b, p = core // 2, core % 2
        own, oth = OWNS[p], OWNS[1 - p]
        rows_own = np.concatenate([np.arange(g * 256, (g + 1) * 256) for g in own])
        rows_oth = np.concatenate([np.arange(g * 256, (g + 1) * 256) for g in oth])
        x_halo = np.zeros((64, D), f32)
        hv = np.zeros((4, 16), f32)
        for s_, g in enumerate(own):
            if g > 0:
                x_halo[s_ * 16:(s_ + 1) * 16] = x[b, g * 256 - 16:g * 256]
                hv[s_] = 1.0
        cst = np.zeros((128, NCST), f32)
        cst[:, O_C:O_C + 16] = c[b].reshape(16, 128).T
        cst[:, O_BADA:O_BADA + 96] = b_ada.reshape(96, 128).T
        cst[:, O_GMIX:O_GMIX + 16] = gmix.reshape(16, 128).T
        cst[:, O_GFFN:O_GFFN + 16] = gffn.reshape(16, 128).T
        cst[:, O_GQ:O_GQ + 128] = gq[None, :]
        cst[:, O_GK:O_GK + 128] = gk[None, :]
        cst[:, O_PSC:O_PSC + 8] = psc.reshape(8, 128).T
        cst[:, O_IFR:O_IFR + 64] = inv_freq[None, :]
        glob = own + oth
        alw = np.zeros((4, 8), f32)
        dg = np.zeros((4, 8), f32)
        for s_ in range(4):
            for kb in range(8):
                alw[s_, kb] = 1.0 if glob[kb] < own[s_] else 0.0
                dg[s_, kb] = 1.0 if kb == s_ else 0.0
        cst[:, O_ALW:O_ALW + 32] = alw.reshape(1, 32)
        cst[:, O_DIAG:O_DIAG + 32] = dg.reshape(1, 32)
        cst[:, O_HV:O_HV + 64] = hv.reshape(1, 64)
        cst[:, O_ID:O_ID + 128] = np.eye(128, dtype=f32)
        cstb = np.zeros((128, NCSTB), f32)
        cstb[:, OB_ID:OB_ID + 128] = np.eye(128, dtype=f32)
        kp = np.arange(128)[:, None, None]
        kt = np.arange(2)[None, :, None]
        q = np.arange(256)[None, None, :]
        cstb[:, OB_TRI:OB_TRI + 512] = np.where(kt * 128 + kp <= q, 0.0, NEG).astype(f32).reshape(128, 512)
        eo = np.zeros((8, 8, 128), f32)
        for j in range(8):
            eo[j, j, :] = 1.0
        cstb[0:8, OB_EOH:OB_EOH + 1024] = eo.reshape(8, 1024)
        pos_all = np.concatenate([positions[b, rows_own], positions[b, rows_oth]])
        pos_l = np.ascontiguousarray(pos_all.reshape(16, 128).T).astype(np.int32)
        t_glob = rows_own.astype(np.int64)
        invcnt = np.stack([1.0 / np.minimum(t_glob + 1, w).astype(f32) for w in POOL_WINDOWS]).astype(f32)
        invcnt_b = np.ascontiguousarray(np.broadcast_to(invcnt[:, None, :], (4, 128, 1024)))
        m = {
            "x_own": np.ascontiguousarray(x[b, rows_own]),
            "x_oth": np.ascontiguousarray(x[b, rows_oth]),
            "x_halo": x_halo,
            "cst": cst, "cstb": cstb, "pos_l": pos_l, "invcnt_b": invcnt_b,
        }
        m.update(shared)
        in_maps.append(m)
    return in_maps


_CACHE = {}


def kernel(x, c, positions, w_ada, b_ada, g_mix_norm, w_in, g_q, g_k, w_pool, pool_scale, w_out,
           g_ffn_norm, w_gate, w_up, w_down):
    in_maps = host_inputs(x, c, positions, w_ada, b_ada, g_mix_norm, w_in, g_q, g_k, w_pool, pool_scale,
                          w_out, g_ffn_norm, w_gate, w_up, w_down)
    key = tuple(DEBUG)
    if key not in _CACHE:
        _CACHE[key] = build_program(DEBUG)
    nc, dbg = _CACHE[key]
    res = run_bass_kernel_spmd(nc, in_maps, core_ids=list(range(8)))
    out = np.empty((NB, SEQ, D), np.float32)
    for core in range(8):
        b, p = core // 2, core % 2
        r = res.results[core]["out"]
        for s_, g in enumerate(OWNS[p]):
            out[b, g * 256:(g + 1) * 256] = r[s_ * 256:(s_ + 1) * 256]
    if DEBUG:
        kernel.last_results = res.results
    return out
```

```python
import numpy as np
from contextlib import ExitStack
import concourse.bass as bass
import concourse.mybir as mybir
from concourse.bass_utils import run_bass_kernel_spmd

F32 = mybir.dt.float32
BF16 = mybir.dt.bfloat16
I32 = mybir.dt.int32
AF = mybir.ActivationFunctionType
ALU = mybir.AluOpType
AX = mybir.AxisListType

D = 2048
SEQ = 2048
NB = 4
H = 8
DH = 128
DFF = 5632
EPS = 1e-6
NEG = -30000.0
POOL_WINDOWS = (2, 4, 8, 16)
OWNS = [[0, 3, 4, 7], [1, 2, 5, 6]]
TWO_PI = float(2 * np.pi)
C1 = 6.28125
C2 = float(2 * np.pi - 6.28125)

O_C, O_BADA, O_GMIX, O_GFFN, O_GQ, O_GK, O_PSC, O_IFR, O_ALW, O_DIAG, O_HV, O_ID = (
    0, 16, 112, 128, 144, 272, 400, 408, 472, 504, 536, 600)
NCST = 728
OB_ID, OB_TRI, OB_EOH = 0, 128, 640
NCSTB = 1664

DEBUG = ()


class Res:
    __slots__ = ("name", "ws", "rs", "ers")

    def __init__(self, name):
        self.name = name
        self.ws = {}
        self.rs = {}
        self.ers = {}


class Sched:
    ENGS = ("pe", "act", "dve", "pool", "sp")

    def __init__(self, nc, stack):
        self.nc = nc
        self.stack = stack
        self.sems = {}
        self.cnt = {}
        self.waited = {e: {} for e in self.ENGS}
        self.prog = {e: [] for e in self.ENGS}
        for e in self.ENGS:
            self.newsem("E_" + e)

    def newsem(self, key):
        self.sems[key] = self.stack.enter_context(self.nc.semaphore(key))
        self.cnt[key] = 0

    def op(self, eng, fn, reads=(), writes=(), awrites=(), dma=None, ndma=1):
        deps = {}

        def add(d):
            for k, v in d.items():
                if deps.get(k, 0) < v:
                    deps[k] = v
        for r in reads:
            add(r.ws)
        for w in writes:
            add(w.ws)
            add(w.rs)
        for a in awrites:
            add(a.rs)
            add(a.ers)
        waits = []
        wd = self.waited[eng]
        for k, v in deps.items():
            if eng == "pe" and k == "E_pe":
                continue
            if wd.get(k, 0) < v:
                wd[k] = v
                waits.append((self.sems[k], v))
        if dma is not None:
            key, inc = dma, 16
            self.cnt[key] += 16 * ndma
        else:
            key, inc = "E_" + eng, 1
            self.cnt[key] += 1
        tok = (key, self.cnt[key])
        semh = self.sems[key]

        def run(e, waits=waits, fn=fn, semh=semh, inc=inc):
            for (s, v) in waits:
                e.wait_ge(s, v)
            r = fn(e)
            if isinstance(r, (list, tuple)):
                for ins in r:
                    ins.then_inc(semh, inc)
            else:
                r.then_inc(semh, inc)
        self.prog[eng].append(run)
        for r in reads:
            if r.rs.get(tok[0], 0) < tok[1]:
                r.rs[tok[0]] = tok[1]
        for w in writes:
            w.ers = dict(w.rs)
            w.ws = {tok[0]: tok[1]}
            w.rs = {}
        for a in awrites:
            if a.ws.get(tok[0], 0) < tok[1]:
                a.ws[tok[0]] = tok[1]
        return tok

    def barrier(self, engs=("pe", "act", "dve", "sp")):
        for eng in engs:
            waits = []
            wd = self.waited[eng]
            for k, v in self.cnt.items():
                if v == 0 or k == "E_" + eng:
                    continue
                if k.startswith("D_ring") or k == "D_out":
                    continue
                if wd.get(k, 0) < v:
                    wd[k] = v
                    waits.append((self.sems[k], v))

            def run(e, waits=waits):
                for (s, v) in waits:
                    e.wait_ge(s, v)
            self.prog[eng].append(run)

    def final_wait(self, eng, key):
        semh, v = self.sems[key], self.cnt[key]
        self.prog[eng].append(lambda e: e.wait_ge(semh, v))


def build_program(debug=()):
    nc = bass.Bass("TRN2", target_bir_lowering=False)

    def din(name, shape, dt=F32):
        return nc.dram_tensor(name, list(shape), dt, kind="ExternalInput").ap()
    x_own = din("x_own", [1024, D])
    x_oth = din("x_oth", [1024, D])
    x_halo = din("x_halo", [64, D])
    cst_d = din("cst", [128, NCST])
    cstb_d = din("cstb", [128, NCSTB])
    pos_d = din("pos_l", [128, 16], I32)
    invcnt_d = din("invcnt_b", [4, 128, 1024])
    w_ada = din("w_ada", [D, 6 * D])
    w_in = din("w_in", [D, 4096])
    w_pool = din("w_pool", [4, 256, 256])
    w_out = din("w_out", [D, D])
    w_gate = din("w_gate", [D, DFF])
    w_up = din("w_up", [D, DFF])
    w_down = din("w_down", [DFF, D])
    out_d = nc.dram_tensor("out", [1024, D], F32, kind="ExternalOutput").ap()
    dbg = {}

    def dbg_out(name, shape, dt=F32):
        dbg[name] = nc.dram_tensor("dbg_" + name, list(shape), dt, kind="ExternalOutput").ap()
        return dbg[name]

    base = (nc.sbuf_base + 63) // 64 * 64
    avail = nc.sbuf_top - base

    def sb(name, shape, dt, off):
        assert off % 32 == 0, (name, off)
        esz = 4 if dt in (F32, I32) else 2
        n = 1
        for s_ in shape[1:]:
            n *= s_
        assert off + n * esz <= avail, (name, off, n * esz, avail)
        return nc.alloc_sbuf_tensor_at(name, list(shape), dt, offset=base + off)

    ring = [sb("ring0", [128, 16, 512], BF16, 0), sb("ring1", [128, 16, 512], BF16, 16384)]
    o = 32768
    cst = sb("cst", [128, NCST], F32, o); o += NCST * 4
    cstb = sb("cstb", [128, NCSTB], BF16, o); o += NCSTB * 2
    wpool = sb("wpool", [128, 4, 2, 256], BF16, o); o += 4096
    ones_bf = sb("ones_bf", [128, 128], BF16, o); o += 256
    ones_f = sb("ones_f", [128, 128], F32, o); o += 512
    posi = sb("posi", [128, 16], I32, o); o += 64
    posf = sb("posf", [128, 16], F32, o); o += 64
    s_f = sb("s_f", [128, 16], F32, o); o += 64
    s_bf = sb("s_bf", [128, 16], BF16, o); o += 64
    modT = sb("modT", [128, 96], F32, o); o += 384
    gm1T = sb("gm1T", [128, 16], F32, o); o += 64
    gm2T = sb("gm2T", [128, 16], F32, o); o += 64
    one_f = sb("one_f", [1, 1], F32, o); o += 64
    kmean_bf = sb("kmean_bf", [128, 32], BF16, o); o += 64
    km_f = sb("km_f", [128, 32], F32, o); o += 128
    ssq = sb("ssq", [128, 32], F32, o); o += 128
    rsq = sb("rsq", [128, 32], F32, o); o += 128
    rstd = sb("rstd", [128, 32], F32, o); o += 128
    pbias = sb("pbias", [128, 32], F32, o); o += 128
    st4 = sb("st4", [128, 16], F32, o); o += 64
    st4b = sb("st4b", [128, 16], F32, o); o += 64
    jq0 = sb("jq0", [128, 512], BF16, o); o += 1024
    assert o <= 47104, o
    BCo = 47104
    bc = sb("bc", [128, 2048], F32, BCo)
    rowst = sb("rowst", [1, 512], F32, BCo)
    PH = 55296
    hT = sb("hT", [128, 16, 2112], BF16, PH)
    concatT = sb("concatT", [128, 16, 1024], BF16, PH + 67584)
    cosT = sb("cosT", [128, 16, 64], F32, PH + 100352)
    sinT = sb("sinT", [128, 16, 64], F32, PH + 104448)
    XR = PH + 108544
    QT = sb("QT", [128, 4, 1024], BF16, XR)
    KT = sb("KT", [128, 4, 2048], BF16, XR + 8192)
    Vt = sb("Vt", [128, 16, 512], BF16, XR + 24576)
    SC = XR + 40960
    sq = sb("sq", [128, 512], F32, SC)
    qn = sb("qn", [128, 512], F32, SC + 2048)
    ta = sb("ta", [128, 512], F32, SC + 4096)
    qrb = sb("qrb", [128, 512], BF16, SC + 6144)
    sq2 = sb("sq2", [128, 512], F32, BCo)
    qn2 = sb("qn2", [128, 512], F32, BCo + 2048)
    ta2 = sb("ta2", [128, 512], F32, BCo + 4096)
    qrb2 = sb("qrb2", [128, 512], BF16, BCo + 6144)
    jq1 = sb("jq1", [128, 512], BF16, BCo + 7168)
    jq = [jq0, jq1]
    biasT = [sb("biasT0", [32, 1024], BF16, SC), sb("biasT1", [32, 1024], BF16, SC + 2048)]
    PT = [sb("PT0", [128, 512], BF16, SC + 4096), sb("PT1", [128, 512], BF16, SC + 5120)]
    rden = sb("rden", [128, 512], F32, BCo + 2048)
    PT2 = sb("PT2", [128, 512], BF16, BCo + 4096)
    gmw = sb("gmw", [128, 64], F32, SC + 7168)
    top8 = sb("top8", [128, 64], F32, SC + 7424)
    selw = sb("selw", [128, 64], F32, SC + 7680)
    xb = [sb("xb0", [128, 2048], F32, PH + 67584), sb("xb1", [128, 2048], F32, PH + 67584 + 8192)]
    xnb = [sb("xn0", [128, 2048], F32, XR + 29696), sb("xn1", [128, 2048], F32, XR + 37888)]
    junk1 = sb("junk1", [128, 2048], BF16, BCo)
    ang = sb("ang", [128, 16, 64], F32, XR + 32768)
    angk = sb("angk", [128, 16, 64], F32, XR + 36864)
    angi = sb("angi", [128, 16, 64], I32, XR + 40960)
    UT = [sb("UT0", [128, 4, 272], F32, XR), sb("UT1", [128, 4, 272], F32, XR + 4352)]
    pa = sb("pa", [128, 4, 272], F32, XR + 8704)
    pbb = sb("pbb", [128, 4, 272], F32, XR + 13056)
    plT = [sb("plT0", [128, 2, 1024], BF16, XR + 17408), sb("plT1", [128, 2, 1024], BF16, XR + 21504)]
    invc = sb("invc", [128, 1024], F32, XR + 25600)
    x1 = sb("x1", [128, 8, 2048], F32, PH)
    gT = sb("gT", [128, 16, 1024], BF16, PH + 67584)
    h2T = sb("h2T", [128, 16, 1024], BF16, PH + 100352)
    fxn = [sb("fxn0", [128, 2048], F32, PH + 133120), sb("fxn1", [128, 2048], F32, PH + 141312)]
    junk2 = sb("junk2", [128, 2048], BF16, PH + 149504)
    sab = [sb("sa0", [128, 512], F32, PH + 133120), sb("sa1", [128, 512], F32, PH + 135168)]
    dtb = [sb("dt0", [128, 512], F32, PH + 137216), sb("dt1", [128, 512], F32, PH + 139264)]
    m4t = [sb("m4t0", [128, 512], F32, SC), sb("m4t1", [128, 512], F32, SC + 2048)]
    dgb = [sb("dg0", [128, 128], F32, SC + 4096), sb("dg1", [128, 128], F32, SC + 4608)]
    assert SC + 8032 <= avail + 1, (SC, avail)

    stack = ExitStack()
    with stack:
        banks = [stack.enter_context(nc.psum_tensor("bank%d" % i, [128, 512], F32)) for i in range(8)]
        S = Sched(nc, stack)
        for k in ("D_cst", "D_cstb", "D_ring0", "D_ring1", "D_xb0", "D_xb1", "D_x1", "D_invc", "D_out", "D_dbg"):
            S.newsem(k)
        Rbank = [Res("bank%d" % i) for i in range(8)]
        bank_bf = [b[:, :].bitcast(BF16) for b in banks]
        Rring = [Res("ring0"), Res("ring1")]

        cident_f = cst[:, O_ID:O_ID + 128]
        ident_bf = cstb[:, OB_ID:OB_ID + 128]
        tri_bf = cstb[:, OB_TRI:OB_TRI + 512].rearrange("p (a q) -> p a q", a=2)
        eoh = cstb[:, OB_EOH:OB_EOH + 1024].rearrange("p (a m) -> p a m", a=8)

        R_cst = Res("cst")
        R_cstb = Res("cstb")
        S.op("sp", lambda e: e.dma_start(out=cst[:, :], in_=cst_d[:, :]), dma="D_cst")
        S.op("sp", lambda e: e.dma_start(out=posi[:, :], in_=pos_d[:, :]), dma="D_cst")
        R_cst.ws = {"D_cst": S.cnt["D_cst"]}
        S.op("pool", lambda e: e.dma_start(out=cstb[:, :], in_=cstb_d[:, :]), dma="D_cstb")
        S.op("pool", lambda e: e.dma_start(
            out=wpool[:, :, :, :], in_=w_pool.rearrange("g (cc p) d -> p g cc d", p=128)), dma="D_cstb")
        R_cstb.ws = {"D_cstb": S.cnt["D_cstb"]}

        R_ones = Res("ones")

        def f_ones(e):
            e.memset(ones_bf[:, :], 1.0)
            e.memset(ones_f[:, :], 1.0)
            return e.memset(one_f[:, :], 1.0)
        S.op("dve", f_ones, writes=[R_ones])

        def wv(ap):
            return ap.rearrange("(kc p) n -> p kc n", p=128)
        wsched = []

        def full(b):
            return b[:, :, :]
        for g in (1, 0):
            for j in range(4):
                wsched.append((("ada", g, j), [(full, wv(w_ada[:, g * D + j * 512:g * D + (j + 1) * 512]))]))

        def win(cg):
            return (("win", cg), [(full, wv(w_in[:, cg * 512:(cg + 1) * 512]))])
        wsched += [win(6), win(7)]
        ada_late = [(2, j) for j in range(4)] + [(3, j) for j in range(4)] + \
                   [(4, j) for j in range(4)] + [(5, j) for j in range(4)]
        for hg in range(2):
            wsched += [win(hg), win(2 + hg), win(4 + hg)]
            for (g, j) in ada_late[hg * 8:(hg + 1) * 8]:
                wsched.append((("ada", g, j), [(full, wv(w_ada[:, g * D + j * 512:g * D + (j + 1) * 512]))]))
        for cg in range(4):
            wsched.append((("wout", cg), [(full, wv(w_out[:, cg * 512:(cg + 1) * 512]))]))
        PHASES = [(0, 8), (8, 16), (16, 22)]
        for (u0, u1) in PHASES:
            for u in range(u0, u1):
                wsched.append((("gu", u), [
                    (lambda b: b[:, :, 0:256], wv(w_gate[:, u * 256:(u + 1) * 256])),
                    (lambda b: b[:, :, 256:512], wv(w_up[:, u * 256:(u + 1) * 256]))]))
            nfc = 2 * (u1 - u0)
            for dg in range(4):
                wsched.append((("wd", u0, dg), [
                    (lambda b, nfc=nfc: b[:, 0:nfc, :],
                     wv(w_down[u0 * 256:u1 * 256, dg * 512:(dg + 1) * 512]))]))
        wstate = {"issued": 0, "next": 0}

        def w_issue():
            n = wstate["issued"]
            if n >= len(wsched):
                return
            b = n % 2
            items = wsched[n][1]

            def f(e, items=items, b=b):
                return [e.dma_start(out=dst(ring[b]), in_=src) for (dst, src) in items]
            S.op("pool", f, writes=[Rring[b]], dma="D_ring%d" % b, ndma=len(items))
            wstate["issued"] = n + 1

        def ring_next(key):
            n = wstate["next"]
            assert wsched[n][0] == key, (wsched[n][0], key)
            while wstate["issued"] <= n:
                w_issue()
            w_issue_after = (wstate["issued"] == n + 1)
            wstate["next"] = n + 1
            return ring[n % 2], Rring[n % 2], w_issue_after

        def ring_prefetch():
            if wstate["issued"] == wstate["next"]:
                w_issue()

        w_issue()
        w_issue()

        def dump(name, ap_fn, shape, dt, reads):
            if name not in debug:
                return
            d = dbg_out(name, shape, dt)
            S.op("sp", lambda e: e.dma_start(out=d, in_=ap_fn()), reads=reads, dma="D_dbg")

        R_sf, R_sbf, R_row = Res("s_f"), Res("s_bf"), Res("rowst")
        R_mod = [Res("mod%d" % g) for g in range(6)]
        S.op("act", lambda e: e.activation(out=s_f[:, :], in_=cst[:, O_C:O_C + 16], func=AF.Silu),
             reads=[R_cst], writes=[R_sf])
        S.op("dve", lambda e: e.tensor_copy(out=s_bf[:, :], in_=s_f[:, :]), reads=[R_sf], writes=[R_sbf])

        def ada_chunk(g, j, b0, b1):
            buf, Rb, _ = ring_next(("ada", g, j))

            def f(e):
                for kc in range(16):
                    ins = e.matmul(banks[b0][0:1, :], lhsT=s_bf[:, kc:kc + 1], rhs=buf[:, kc, :],
                                   start=(kc == 0), stop=(kc == 15))
                return ins
            S.op("pe", f, reads=[Rb, R_sbf], writes=[Rbank[b0]])
            ring_prefetch()
            S.op("dve", lambda e: e.tensor_copy(out=rowst[0:1, :], in_=banks[b0][0:1, :]),
                 reads=[Rbank[b0]], writes=[R_row])

            def f2(e):
                for q in range(4):
                    ins = e.matmul(banks[b1][:, q:q + 1], lhsT=rowst[0:1, q * 128:(q + 1) * 128],
                                   rhs=one_f[0:1, 0:1], start=True, stop=True)
                return ins
            S.op("pe", f2, reads=[R_row, R_ones], writes=[Rbank[b1]])
            col = g * 16 + j * 4
            S.op("dve", lambda e: e.tensor_tensor(out=modT[:, col:col + 4], in0=banks[b1][:, 0:4],
                                                  in1=cst[:, O_BADA + col:O_BADA + col + 4], op=ALU.add),
                 reads=[Rbank[b1], R_cst], awrites=[R_mod[g]])

        for g in (1, 0):
            for j in range(4):
                ada_chunk(g, j, 0 + (j % 2), 2 + (j % 2))
        R_gm1, R_gm2 = Res("gm1"), Res("gm2")
        S.op("dve", lambda e: e.scalar_tensor_tensor(out=gm1T[:, :], in0=modT[:, 16:32], scalar=1.0,
                                                     in1=cst[:, O_GMIX:O_GMIX + 16], op0=ALU.add, op1=ALU.mult),
             reads=[R_mod[1], R_cst], writes=[R_gm1])
        dump("modT01", lambda: modT[:, 0:32], [128, 32], F32, [R_mod[0], R_mod[1]])

        R_rope = Res("rope")
        R_ang, R_angk, R_angi, R_posf = Res("ang"), Res("angk"), Res("angi"), Res("posf")
        A3 = lambda t: t[:, :, :]
        S.op("dve", lambda e: e.tensor_copy(out=posf[:, :], in_=posi[:, :]), reads=[R_cst], writes=[R_posf])
        S.op("dve", lambda e: e.tensor_tensor(
            out=A3(ang), in0=posf[:, :].unsqueeze(2).to_broadcast([128, 16, 64]),
            in1=cst[:, O_IFR:O_IFR + 64].unsqueeze(1).to_broadcast([128, 16, 64]), op=ALU.mult),
            reads=[R_posf, R_cst], writes=[R_ang])
        S.op("dve", lambda e: e.tensor_scalar(out=A3(angi), in0=A3(ang), scalar1=float(1.0 / TWO_PI), scalar2=None,
                                              op0=ALU.mult), reads=[R_ang], writes=[R_angi])
        S.op("dve", lambda e: e.tensor_copy(out=A3(angk), in_=A3(angi)), reads=[R_angi], writes=[R_angk])
        S.op("dve", lambda e: e.scalar_tensor_tensor(out=A3(ang), in0=A3(angk), scalar=-C1, in1=A3(ang),
                                                     op0=ALU.mult, op1=ALU.add), reads=[R_angk, R_ang], writes=[R_ang])
        S.op("dve", lambda e: e.scalar_tensor_tensor(out=A3(ang), in0=A3(angk), scalar=-C2, in1=A3(ang),
                                                     op0=ALU.mult, op1=ALU.add), reads=[R_angk, R_ang], writes=[R_ang])

        def wrap(lo_hi):
            for (thr, op_, add_) in lo_hi:
                S.op("dve", lambda e, thr=thr, op_=op_: e.tensor_single_scalar(out=A3(angk), in_=A3(ang), scalar=thr, op=op_),
                     reads=[R_ang], writes=[R_angk])
                S.op("dve", lambda e, add_=add_: e.scalar_tensor_tensor(out=A3(ang), in0=A3(angk), scalar=add_, in1=A3(ang),
                                                                        op0=ALU.mult, op1=ALU.add),
                     reads=[R_angk, R_ang], writes=[R_ang])
        wrap([(float(np.pi), ALU.is_gt, -TWO_PI), (float(-np.pi), ALU.is_lt, TWO_PI)])
        S.op("act", lambda e: e.activation(out=sinT[:, :, :], in_=A3(ang), func=AF.Sin),
             reads=[R_ang], awrites=[R_rope])
        S.op("dve", lambda e: e.tensor_scalar(out=A3(ang), in0=A3(ang), scalar1=float(np.pi / 2), scalar2=None, op0=ALU.add),
             reads=[R_ang], writes=[R_ang])
        wrap([(float(np.pi), ALU.is_gt, -TWO_PI)])
        S.op("act", lambda e: e.activation(out=cosT[:, :, :], in_=A3(ang), func=AF.Sin),
             reads=[R_ang], awrites=[R_rope])
        R_pb = Res("pbias")
        S.op("dve", lambda e: e.tensor_scalar(out=pbias[:, :], in0=cst[:, O_ALW:O_ALW + 32], scalar1=-NEG,
                                              scalar2=NEG, op0=ALU.mult, op1=ALU.add),
             reads=[R_cst], writes=[R_pb])
        S.barrier()

        R_xb = [Res("xb0"), Res("xb1")]
        R_xn = [Res("xn0"), Res("xn1")]
        R_hT = [Res("hT%d" % i) for i in range(17)]
        R_stc = [Res("st%d" % i) for i in range(32)]

        def norm_A(src_ap, np_, xbuf, Rx, xnbuf, Rxn, col, junk, Rjunk, x_is_sbuf=False):
            R_st = R_stc[col]
            if not x_is_sbuf:
                S.op("sp", lambda e: e.dma_start(out=xbuf[0:np_, :], in_=src_ap), writes=[Rx],
                     dma="D_" + Rx.name)
                xin = xbuf[0:np_, :]
            else:
                xin = src_ap
            S.op("act", lambda e: e.activation(out=junk[0:np_, :], in_=xin, func=AF.Square,
                                               accum_out=ssq[0:np_, col:col + 1]),
                 reads=[Rx], writes=[Rjunk, R_st])
            S.op("act", lambda e: e.activation(out=rsq[0:np_, col:col + 1], in_=ssq[0:np_, col:col + 1],
                                               func=AF.Sqrt, scale=1.0 / D, bias=EPS),
                 reads=[R_st], writes=[R_st])
            S.op("dve", lambda e: e.reciprocal(out=rstd[0:np_, col:col + 1], in_=rsq[0:np_, col:col + 1]),
                 reads=[R_st], writes=[R_st])
            S.op("act", lambda e: e.activation(out=xnbuf[0:np_, :], in_=xin, func=AF.Copy,
                                               scale=rstd[0:np_, col:col + 1]),
                 reads=[Rx, R_st], writes=[Rxn])

        def norm_B(np_, xnbuf, Rxn, gmT, Rgm, shT, Rsh, dst_fn, Rdst, bank0):
            for q4 in range(4):
                bk = bank0 + q4

                def ft(e, q4=q4, bk=bk):
                    for q in range(4):
                        kc = q4 * 4 + q
                        ins = e.transpose(banks[bk][:, q * 128:q * 128 + np_], xnbuf[0:np_, kc * 128:(kc + 1) * 128],
                                          cident_f[0:np_, 0:np_])
                    return ins
                S.op("pe", ft, reads=[Rxn, R_cst], writes=[Rbank[bk]])
                for q in range(4):
                    kc = q4 * 4 + q
                    S.op("dve", lambda e, q=q, kc=kc, bk=bk: e.tensor_scalar(
                        out=dst_fn(kc), in0=banks[bk][:, q * 128:q * 128 + np_],
                        scalar1=gmT[:, kc:kc + 1], scalar2=shT[:, kc:kc + 1], op0=ALU.mult, op1=ALU.add),
                        reads=[Rbank[bk], Rgm, Rsh], awrites=[Rdst])

        sh1T = modT[:, 0:16]
        R_junk = Res("junk")

        def m1_A(pos):
            i = m1_seq[pos]
            if i < 8:
                src = x_own[i * 128:(i + 1) * 128, :]
            elif i < 16:
                src = x_oth[(i - 8) * 128:(i - 7) * 128, :]
            else:
                src = x_halo[:, :]
            np_ = 128 if i < 16 else 64
            norm_A(src, np_, xb[pos % 2], R_xb[pos % 2], xnb[pos % 2], R_xn[pos % 2], i, junk1, R_junk)

        def m1_B(pos):
            i = m1_seq[pos]
            np_ = 128 if i < 16 else 64
            t0 = i * 128
            bank0 = 4 * (pos % 2) if pos < 9 else 4
            norm_B(np_, xnb[pos % 2], R_xn[pos % 2], gm1T, R_gm1, sh1T, R_mod[0],
                   lambda kc, t0=t0, np_=np_: hT[:, kc, t0:t0 + np_], R_hT[i], bank0)
        m1_seq = list(range(8)) + [16] + list(range(8, 16))

        R_UT = [Res("UT0"), Res("UT1")]
        R_pa, R_pb2 = Res("pa"), Res("pbb")
        R_pl = [Res("pl0"), Res("pl1")]
        R_invc = Res("invc")
        R_cT = [Res("cT%d" % s_) for s_ in range(4)]
        hv_b = cst[:, O_HV:O_HV + 64].rearrange("p (s t) -> p s t", s=4)
        poolA = [0, 1, 2, 3]
        pa_i = [0]

        def nextbank(pool_, ctr):
            b = pool_[ctr[0] % len(pool_)]
            ctr[0] += 1
            return b
        pend_pool = []

        def pool_mm(g, pl):
            for dc in range(2):
                for th in range(2):
                    bk = nextbank(poolA, pa_i)

                    def fp(e, bk=bk, dc=dc, th=th):
                        for c2 in range(2):
                            ins = e.matmul(banks[bk][:, :], lhsT=wpool[:, g, c2, dc * 128:(dc + 1) * 128],
                                           rhs=pl[:, c2, th * 512:(th + 1) * 512], start=(c2 == 0), stop=(c2 == 1))
                        return ins
                    S.op("pe", fp, reads=[R_pl[g % 2], R_cstb], writes=[Rbank[bk]])
                    ch = g * 2 + dc
                    S.op("act", lambda e, bk=bk, ch=ch, th=th: e.activation(
                        out=concatT[:, 8 + ch, th * 512:(th + 1) * 512], in_=banks[bk][:, :], func=AF.Copy,
                        scale=cst[:, O_PSC + ch:O_PSC + ch + 1]),
                        reads=[Rbank[bk], R_cst], awrites=[R_cT[2 * th], R_cT[2 * th + 1]])

        t2s = {"buf": None, "Rb": None}

        def t2_chunk(c8):
            if True:
                cgu, cc = c8 // 4, c8 % 4
                first = (cc == 0)
                if first:
                    t2s["buf"], t2s["Rb"], _ = ring_next(("win", 6 + cgu))
                buf, Rb = t2s["buf"], t2s["Rb"]
                g = c8 // 2
                w = POOL_WINDOWS[g]
                ub = c8 % 2
                if c8 % 2 == 0:
                    S.op("sp", lambda e, g=g: e.dma_start(out=invc[:, :], in_=invcnt_d[g]), writes=[R_invc],
                         dma="D_invc")
                for th in range(2):
                    bk = nextbank(poolA, pa_i)

                    def f(e, bk=bk, cc=cc, th=th, buf=buf):
                        for kc in range(16):
                            ins = e.matmul(banks[bk][:, :], lhsT=buf[:, kc, cc * 128:(cc + 1) * 128],
                                           rhs=hT[:, kc, th * 512:(th + 1) * 512], start=(kc == 0), stop=(kc == 15))
                        return ins
                    S.op("pe", f, reads=[Rb] + R_hT[4 * th:4 * th + 4], writes=[Rbank[bk]])
                    if first:
                        ring_prefetch()
                        first = False
                    S.op("act", lambda e, bk=bk, ub=ub, th=th: e.activation(
                        out=UT[ub][:, 2 * th:2 * th + 2, 16:272],
                        in_=banks[bk][:, :].rearrange("p (s t) -> p s t", s=2), func=AF.Copy),
                        reads=[Rbank[bk]], awrites=[R_UT[ub]])
                bk = nextbank(poolA, pa_i)

                def fh(e, bk=bk, cc=cc, buf=buf):
                    for kc in range(16):
                        ins = e.matmul(banks[bk][:, 0:64], lhsT=buf[:, kc, cc * 128:(cc + 1) * 128],
                                       rhs=hT[:, kc, 2048:2112], start=(kc == 0), stop=(kc == 15))
                    return ins
                S.op("pe", fh, reads=[Rb, R_hT[16]], writes=[Rbank[bk]])
                S.op("dve", lambda e, bk=bk, ub=ub: e.tensor_tensor(
                    out=UT[ub][:, :, 0:16], in0=banks[bk][:, 0:64].rearrange("p (s t) -> p s t", s=4),
                    in1=hv_b, op=ALU.mult), reads=[Rbank[bk], R_cst], awrites=[R_UT[ub]])
                src_t, Rsrc = UT[ub], R_UT[ub]
                step = 1
                vs = 0
                while step < w:
                    dst_t, Rd = (pa, R_pa) if src_t is not pa else (pbb, R_pb2)
                    S.op("dve", lambda e, s_=src_t, d_=dst_t, step=step, vs=vs: e.tensor_tensor(
                        out=d_[:, :, vs + step:272], in0=s_[:, :, vs + step:272], in1=s_[:, :, vs:272 - step], op=ALU.add),
                        reads=[Rsrc], writes=[Rd])
                    src_t, Rsrc = dst_t, Rd
                    vs += step
                    step *= 2
                tmp_t, Rt = (pa, R_pa) if src_t is not pa else (pbb, R_pb2)
                S.op("dve", lambda e, s_=src_t, t_=tmp_t: e.tensor_tensor(
                    out=t_[:, :, 16:272], in0=s_[:, :, 16:272],
                    in1=invc[:, :].rearrange("p (s t) -> p s t", s=4), op=ALU.mult),
                    reads=[Rsrc, R_invc], writes=[Rt])
                pl = plT[g % 2]
                S.op("dve", lambda e, t_=tmp_t, ub=ub, pl=pl, c8=c8: e.tensor_tensor(
                    out=pl[:, c8 % 2, :].rearrange("p (s t) -> p s t", s=4), in0=t_[:, :, 16:272],
                    in1=UT[ub][:, :, 16:272], op=ALU.subtract),
                    reads=[Rt, R_UT[ub]], awrites=[R_pl[g % 2]])
                if pend_pool:
                    pool_mm(*pend_pool.pop(0))
                if c8 % 2 == 1:
                    pend_pool.append((g, pl))

        m1_A(0)
        for pos in range(9):
            m1_A(pos + 1)
            m1_B(pos)
        for c8 in range(8):
            pos = 9 + c8
            if pos + 1 < 17:
                m1_A(pos + 1)
            t2_chunk(c8)
            m1_B(pos)
        while pend_pool:
            pool_mm(*pend_pool.pop(0))
        dump("hT", lambda: hT[:, :, :], [128, 16, 2112], BF16, R_hT)
        dump("poolT", lambda: concatT[:, 8:16, :], [128, 8, 1024], BF16, R_cT)
        S.barrier()

        R_QT = [Res("QT%d" % i) for i in range(8)]
        R_KT = [Res("KT%d" % i) for i in range(16)]
        R_V = [Res("V%d" % i) for i in range(16)]
        R_km = Res("kmean")
        poolP = [0, 1, 2, 5, 6]
        poolTr = [3, 4]
        pp_i, pt_i = [0], [0]
        gq_b = cst[:, O_GQ:O_GQ + 128]
        gk_b = cst[:, O_GK:O_GK + 128]

        def v4(t):
            return t.rearrange("p (h d) -> p h d", h=4)

        def v5(t):
            return t.rearrange("p (h a d) -> p h a d", h=4, a=2)

        SCR = [dict(jq=jq[0], R_jq=Res("jq0"), sq=sq, qn=qn, ta=ta, qrb=qrb, st=st4, R_sq=Res("sq"), R_qn=Res("qn"), R_ta=Res("ta"),
                    R_qrb=Res("qrb"), R_s4=Res("st4")),
               dict(jq=jq[1], R_jq=Res("jq1"), sq=sq2, qn=qn2, ta=ta2, qrb=qrb2, st=st4b, R_sq=Res("sq2"), R_qn=Res("qn2"), R_ta=Res("ta2"),
                    R_qrb=Res("qrb2"), R_s4=Res("st4b"))]

        def qk_part1(bk, i, g_b, sc):
            sq_, qn_, ta_, qrb_, st_ = sc["sq"], sc["qn"], sc["ta"], sc["qrb"], sc["st"]
            R_sq_, R_qn_, R_ta_, R_qrb_, R_s4_ = sc["R_sq"], sc["R_qn"], sc["R_ta"], sc["R_qrb"], sc["R_s4"]

            jq_ = sc["jq"]

            def fsq(e):
                for h in range(4):
                    ins = e.activation(out=jq_[:, h * 128:(h + 1) * 128], in_=banks[bk][:, h * 128:(h + 1) * 128],
                                       func=AF.Square, accum_out=st_[:, h:h + 1])
                return ins
            S.op("act", fsq, reads=[Rbank[bk]], writes=[sc["R_jq"], R_s4_])
            S.op("act", lambda e: e.activation(out=st_[:, 4:8], in_=st_[:, 0:4], func=AF.Sqrt, scale=1.0 / DH, bias=EPS),
                 reads=[R_s4_], writes=[R_s4_])
            S.op("dve", lambda e: e.reciprocal(out=st_[:, 8:12], in_=st_[:, 4:8]), reads=[R_s4_], writes=[R_s4_])

        def qk_part1b(bk, i, g_b, sc):
            sq_, qn_, ta_, qrb_, st_ = sc["sq"], sc["qn"], sc["ta"], sc["qrb"], sc["st"]
            R_sq_, R_qn_, R_ta_, R_qrb_, R_s4_ = sc["R_sq"], sc["R_qn"], sc["R_ta"], sc["R_qrb"], sc["R_s4"]

            def fqn(e):
                for h in range(4):
                    ins = e.activation(out=qn_[:, h * 128:(h + 1) * 128], in_=banks[bk][:, h * 128:(h + 1) * 128],
                                       func=AF.Copy, scale=st_[:, 8 + h:9 + h])
                return ins
            S.op("act", fqn, reads=[Rbank[bk], R_s4_], writes=[R_qn_])
            S.op("dve", lambda e: e.tensor_tensor(
                out=v4(qn_[:, :]), in0=v4(qn_[:, :]), in1=g_b.unsqueeze(1).to_broadcast([128, 4, 128]), op=ALU.mult),
                reads=[R_qn_, R_cst], writes=[R_qn_])
            cs_b = cosT[:, i, :].unsqueeze(1).unsqueeze(1).to_broadcast([128, 4, 2, 64])
            sn_b = sinT[:, i, :].unsqueeze(1).to_broadcast([128, 4, 64])
            S.op("dve", lambda e: e.tensor_tensor(out=v5(ta_[:, :]), in0=v5(qn_[:, :]), in1=cs_b, op=ALU.mult),
                 reads=[R_qn_, R_rope], writes=[R_ta_])

            def fb(e):
                e.tensor_tensor(out=v5(sq_[:, :])[:, :, 0, :], in0=v5(qn_[:, :])[:, :, 1, :], in1=sn_b, op=ALU.mult)
                return e.tensor_tensor(out=v5(sq_[:, :])[:, :, 1, :], in0=v5(qn_[:, :])[:, :, 0, :], in1=sn_b, op=ALU.mult)
            S.op("dve", fb, reads=[R_qn_, R_rope], writes=[R_sq_])

            def fo(e):
                e.tensor_tensor(out=v5(qrb_[:, :])[:, :, 0, :], in0=v5(ta_[:, :])[:, :, 0, :],
                                in1=v5(sq_[:, :])[:, :, 0, :], op=ALU.subtract)
                return e.tensor_tensor(out=v5(qrb_[:, :])[:, :, 1, :], in0=v5(ta_[:, :])[:, :, 1, :],
                                       in1=v5(sq_[:, :])[:, :, 1, :], op=ALU.add)
            S.op("dve", fo, reads=[R_ta_, R_sq_], writes=[R_qrb_])

        def qk_part2(i, sc, dstT, Rdst):
            qrb_, R_qrb_ = sc["qrb"], sc["R_qrb"]
            tb = nextbank(poolTr, pt_i)

            def ftr(e):
                for h in range(4):
                    ins = e.transpose(bank_bf[tb][:, h * 128:(h + 1) * 128], qrb_[:, h * 128:(h + 1) * 128], ident_bf)
                return ins
            S.op("pe", ftr, reads=[R_qrb_, R_cstb], writes=[Rbank[tb]])
            S.op("act", lambda e: e.activation(out=dstT[:, :, i * 128:(i + 1) * 128],
                                               in_=v4(bank_bf[tb][:, 0:512]), func=AF.Copy),
                 reads=[Rbank[tb]], writes=[Rdst])

        poolS = [0, 1, 2]
        ps_i = [0]
        R_bT = [Res("biasT0"), Res("biasT1")]
        R_PT = [Res("PT0"), Res("PT1"), Res("PT2")]
        PT.append(PT2)
        R_gw, R_t8, R_sel, R_rden = Res("gmw"), Res("top8"), Res("selw"), Res("rden")
        alw4 = cst[:, O_ALW:O_ALW + 32].rearrange("p (s k) -> p s k", s=4)
        diag4 = cst[:, O_DIAG:O_DIAG + 32].rearrange("p (s k) -> p s k", s=4)
        pb4 = pbias[:, :].rearrange("p (s k) -> p s k", s=4)

        def v428(t):
            return t.rearrange("p (s a k) -> p s a k", s=4, a=2)

        OB, DBK = (3, 5), (4, 6)

        def attn_core(hl, hb, h):
            tiles = []
            for j in range(4):
                q0 = j * 256
                chunks = []
                if q0 < 512:
                    chunks.append((q0, 512))
                chunks.append((max(q0, 512), 1024))
                for typ in range(2):
                    kb = j + 4 * typ
                    for (qa, qb) in chunks:
                        for kt in range(2):
                            tiles.append((j, typ, kb, qa, qb, kt))
            ntile = len(tiles)
            first_seen = {}
            last_seen = {}
            for t_i, (j, typ, kb, qa, qb, kt) in enumerate(tiles):
                bsel = 0 if qa < 512 else 1
                first_seen.setdefault(bsel, t_i)
                last_seen[bsel] = t_i

            def emit_s(t_i):
                j, typ, kb, qa, qb, kt = tiles[t_i]
                n = qb - qa
                sbk = nextbank(poolS, ps_i)
                ktile = kb * 2 + kt
                has_diag = (typ == 0) and (qa <= j * 256 < qb)

                def fs(e):
                    o_ = banks[sbk][:, 0:n]
                    e.matmul(o_, lhsT=KT[:, hl, ktile * 128:(ktile + 1) * 128], rhs=QT[:, hl, qa:qb], start=True, stop=False)
                    ins = e.matmul(o_, lhsT=eoh[0:32, kb, :], rhs=biasT[hb][0:32, qa:qb], start=False, stop=(not has_diag))
                    if has_diag:
                        d0 = j * 256 - qa
                        ins = e.matmul(banks[sbk][:, d0:d0 + 256], lhsT=ident_bf, rhs=tri_bf[:, kt, :], start=False, stop=True)
                    return ins
                S.op("pe", fs, reads=[R_KT[ktile], R_bT[hb], R_cstb] + R_QT[qa // 128:qb // 128], writes=[Rbank[sbk]])
                pb_i = t_i % 3
                S.op("act", lambda e: e.activation(out=PT[pb_i][:, 0:n], in_=banks[sbk][:, 0:n], func=AF.Exp,
                                                   scale=float(DH ** -0.5)),
                     reads=[Rbank[sbk]], writes=[R_PT[pb_i]])
                return (t_i, pb_i)

            def emit_pv(t_i, pb_i):
                j, typ, kb, qa, qb, kt = tiles[t_i]
                n = qb - qa
                bsel = 0 if qa < 512 else 1
                ob, db = OB[bsel], DBK[bsel]
                c0 = qa - 512 * bsel
                ktile = kb * 2 + kt
                first = (first_seen[bsel] == t_i)
                last = (last_seen[bsel] == t_i)

                def fpv(e):
                    e.matmul(banks[ob][:, c0:c0 + n], lhsT=Vt[:, ktile, hl * 128:(hl + 1) * 128], rhs=PT[pb_i][:, 0:n],
                             start=first, stop=last)
                    return e.matmul(banks[db][:, c0:c0 + n], lhsT=ones_bf[:, :], rhs=PT[pb_i][:, 0:n], start=first, stop=last)
                wr = [Rbank[ob], Rbank[db]] if first else []
                aw = [] if first else [Rbank[ob], Rbank[db]]
                S.op("pe", fpv, reads=[R_PT[pb_i], R_V[ktile], R_ones], writes=wr, awrites=aw)
            pend = []
            for t_i in range(ntile):
                cur = emit_s(t_i)
                if len(pend) >= 2:
                    emit_pv(*pend.pop(0))
                pend.append(cur)
            while pend:
                emit_pv(*pend.pop(0))
            for bsel in range(2):
                ob, db = OB[bsel], DBK[bsel]
                S.op("dve", lambda e, db=db: e.reciprocal(out=rden[:, :], in_=banks[db][:, :]),
                     reads=[Rbank[db]], writes=[R_rden])
                S.op("dve", lambda e, ob=ob, bsel=bsel: e.tensor_tensor(
                    out=concatT[:, h, bsel * 512:(bsel + 1) * 512], in0=banks[ob][:, :], in1=rden[:, :], op=ALU.mult),
                    reads=[Rbank[ob], R_rden], awrites=[R_cT[2 * bsel], R_cT[2 * bsel + 1]])

        def gate_sel(hl):

            def fg(e):
                for qt in range(8):
                    ins = e.matmul(banks[7][:, qt * 8:(qt + 1) * 8], lhsT=QT[:, hl, qt * 128:(qt + 1) * 128],
                                   rhs=kmean_bf[:, hl * 8:(hl + 1) * 8], start=True, stop=True)
                return ins
            S.op("pe", fg, reads=R_QT + [R_km], writes=[Rbank[7]])
            S.op("dve", lambda e: e.tensor_tensor(
                out=v428(gmw[:, :]), in0=v428(banks[7][:, 0:64]),
                in1=pb4.unsqueeze(2).to_broadcast([128, 4, 2, 8]), op=ALU.add),
                reads=[Rbank[7], R_pb], writes=[R_gw])

            def fmax(e):
                for qt in range(8):
                    ins = e.max(out=top8[:, qt * 8:(qt + 1) * 8], in_=gmw[:, qt * 8:(qt + 1) * 8])
                return ins
            S.op("dve", fmax, reads=[R_gw], writes=[R_t8])

            S.op("dve", lambda e: e.tensor_tensor(
                out=selw[:, :].rearrange("p (q k) -> p q k", q=8), in0=gmw[:, :].rearrange("p (q k) -> p q k", q=8),
                in1=top8[:, :].rearrange("p (q k) -> p q k", q=8)[:, :, 2:3].to_broadcast([128, 8, 8]), op=ALU.is_ge),
                reads=[R_gw, R_t8], writes=[R_sel])
            S.op("dve", lambda e: e.tensor_tensor(out=v428(selw[:, :]), in0=v428(selw[:, :]),
                                                  in1=alw4.unsqueeze(2).to_broadcast([128, 4, 2, 8]), op=ALU.mult),
                 reads=[R_sel, R_cst], writes=[R_sel])
            S.op("dve", lambda e: e.tensor_tensor(out=v428(selw[:, :]), in0=v428(selw[:, :]),
                                                  in1=diag4.unsqueeze(2).to_broadcast([128, 4, 2, 8]), op=ALU.add),
                 reads=[R_sel, R_cst], writes=[R_sel])
            S.op("dve", lambda e: e.tensor_scalar(out=selw[:, :], in0=selw[:, :], scalar1=-NEG, scalar2=NEG,
                                                  op0=ALU.mult, op1=ALU.add), reads=[R_sel], writes=[R_sel])

        def bias_T(hl):
            hb = hl % 2

            def bias_half(half):
                def ftb(e):
                    for q in range(4):
                        qt = half * 4 + q
                        ins = e.transpose(banks[7][0:8, q * 128:(q + 1) * 128], selw[:, qt * 8:(qt + 1) * 8], cident_f)
                    return ins
                S.op("pe", ftb, reads=[R_sel, R_cst], writes=[Rbank[7]])
                S.op("act", lambda e: e.activation(out=biasT[hb][0:8, half * 512:(half + 1) * 512],
                                                   in_=banks[7][0:8, :], func=AF.Copy),
                     reads=[Rbank[7]], writes=[R_bT[hb]] if half == 0 else [], awrites=[] if half == 0 else [R_bT[hb]])
            bias_half(0)
            bias_half(1)

        def attention(hg, ada_list):
            for hb_ in range(2):
                S.op("dve", lambda e, hb_=hb_: e.memset(biasT[hb_][0:32, :], 0.0), writes=[R_bT[hb_]])
            gate_sel(0)
            bias_T(0)
            for hl in range(4):
                if hl + 1 < 4:
                    gate_sel(hl + 1)
                attn_core(hl, hl % 2, hg * 4 + hl)
                if hl + 1 < 4:
                    bias_T(hl + 1)
                for (g, j) in ada_list[hl * 2:hl * 2 + 2]:
                    ada_chunk(g, j, 7, 7)

        for hg in range(2):
            tl = [("q", i, hg) for i in range(8)] + [("k", i, 2 + hg) for i in range(16)] + \
                 [("v", i, 4 + hg) for i in range(16)]
            info = {}
            cur = {"cg": None, "buf": None, "Rb": None}

            def stage_M(t):
                which, i, cg = tl[t]
                if cur["cg"] != cg:
                    cur["buf"], cur["Rb"], _ = ring_next(("win", cg))
                    cur["cg"] = cg
                    newc = True
                else:
                    newc = False
                buf, Rb = cur["buf"], cur["Rb"]
                bk = nextbank(poolP, pp_i)

                def f(e):
                    for kc in range(16):
                        ins = e.matmul(banks[bk][:, :], lhsT=hT[:, kc, i * 128:(i + 1) * 128], rhs=buf[:, kc, :],
                                       start=(kc == 0), stop=(kc == 15))
                    return ins
                S.op("pe", f, reads=[Rb, R_hT[i]], writes=[Rbank[bk]])
                if newc:
                    ring_prefetch()
                sc = None
                if which != "v":
                    sc = SCR[tcount[0] % 2]
                    tcount[0] += 1
                info[t] = (bk, sc)

            def stage_A(t):
                which, i, cg = tl[t]
                bk, sc = info[t]
                if which == "v":
                    S.op("act", lambda e: e.activation(out=Vt[:, i, :], in_=banks[bk][:, :], func=AF.Copy),
                         reads=[Rbank[bk]], writes=[R_V[i]])
                else:
                    qk_part1(bk, i, None, sc)

            def stage_B(t):
                which, i, cg = tl[t]
                bk, sc = info[t]
                if which != "v":
                    qk_part1b(bk, i, gq_b if which == "q" else gk_b, sc)

            def stage_C(t):
                which, i, cg = tl[t]
                bk, sc = info[t]
                if which == "q":
                    qk_part2(i, sc, QT, R_QT[i])
                elif which == "k":
                    qk_part2(i, sc, KT, R_KT[i])
            tcount = [0]
            nt = len(tl)
            for t in range(nt + 3):
                if t < nt:
                    stage_M(t)
                if 0 <= t - 1 < nt:
                    stage_A(t - 1)
                if 0 <= t - 2 < nt:
                    stage_B(t - 2)
                if 0 <= t - 3 < nt:
                    stage_C(t - 3)
            S.op("dve", lambda e: e.tensor_reduce(
                out=km_f[:, :], in_=KT[:, :, :].rearrange("p h (k t) -> p (h k) t", k=8), axis=AX.X, op=ALU.add),
                reads=R_KT, writes=[R_km])
            S.op("dve", lambda e: e.tensor_scalar(out=kmean_bf[:, :], in0=km_f[:, :], scalar1=1.0 / 256.0, scalar2=None,
                                                  op0=ALU.mult), reads=[R_km], writes=[R_km])
            if hg == 0:
                dump("QT0", lambda: QT[:, :, :], [128, 4, 1024], BF16, R_QT)
                dump("KT0", lambda: KT[:, :, :], [128, 4, 2048], BF16, R_KT)
                dump("V0", lambda: Vt[:, :, :], [128, 16, 512], BF16, R_V)
            S.barrier()
            if hg == 1:
                R_x1 = [Res("x1_%d" % i) for i in range(8)]
                for i in range(8):
                    S.op("sp", lambda e, i=i: e.dma_start(out=x1[:, i, :], in_=x_own[i * 128:(i + 1) * 128, :]),
                         dma="D_x1")
                for i in range(8):
                    R_x1[i].ws = {"D_x1": S.cnt["D_x1"]}
            attention(hg, ada_late[hg * 8:(hg + 1) * 8])
            if hg == 0:
                dump("att0", lambda: concatT[:, 0:4, :], [128, 4, 1024], BF16, R_cT)
            S.barrier()
        dump("concatT", lambda: concatT[:, :, :], [128, 16, 1024], BF16, R_cT)
        dump("modT", lambda: modT[:, :], [128, 96], F32, R_mod)

        R_bc = Res("bc")
        R_dg = [Res("dg0"), Res("dg1")]

        def build_bc(g):
            first = True
            for c4 in range(4):
                bk = 3 + (c4 % 2)
                for q in range(4):
                    c = c4 * 4 + q
                    d_ = c % 2
                    S.op("dve", lambda e, c=c, d_=d_: e.tensor_scalar(
                        out=dgb[d_][:, :], in0=cident_f, scalar1=modT[:, g * 16 + c:g * 16 + c + 1], scalar2=None,
                        op0=ALU.mult), reads=[R_cst, R_mod[g]], writes=[R_dg[d_]])
                    S.op("pe", lambda e, bk=bk, q=q, d_=d_: e.matmul(
                        banks[bk][:, q * 128:(q + 1) * 128], lhsT=ones_f[:, :], rhs=dgb[d_][:, :], start=True, stop=True),
                        reads=[R_dg[d_], R_ones], writes=[Rbank[bk]] if q == 0 else [], awrites=[] if q == 0 else [Rbank[bk]])
                if first:
                    S.op("act", lambda e, bk=bk, c4=c4: e.activation(out=bc[:, c4 * 512:(c4 + 1) * 512], in_=banks[bk][:, :],
                                                                     func=AF.Copy), reads=[Rbank[bk]], writes=[R_bc])
                    first = False
                else:
                    S.op("act", lambda e, bk=bk, c4=c4: e.activation(out=bc[:, c4 * 512:(c4 + 1) * 512], in_=banks[bk][:, :],
                                                                     func=AF.Copy), reads=[Rbank[bk]], awrites=[R_bc])

        build_bc(2)
        R_m4t = [Res("m4t0"), Res("m4t1")]
        poolO = [0, 1, 2, 5, 6, 7]
        po_i = [0]
        mt_i = 0
        for cg in range(4):
            buf, Rb, _ = ring_next(("wout", cg))
            for i in range(8):
                bk = nextbank(poolO, po_i)

                def f(e, bk=bk, i=i, buf=buf):
                    for kc in range(16):
                        ins = e.matmul(banks[bk][:, :], lhsT=concatT[:, kc, i * 128:(i + 1) * 128], rhs=buf[:, kc, :],
                                       start=(kc == 0), stop=(kc == 15))
                    return ins
                S.op("pe", f, reads=[Rb, R_cT[i // 2]], writes=[Rbank[bk]])
                if i == 0:
                    ring_prefetch()
                mt = mt_i % 2
                mt_i += 1
                S.op("dve", lambda e, bk=bk, mt=mt, cg=cg: e.tensor_tensor(
                    out=m4t[mt][:, :], in0=banks[bk][:, :], in1=bc[:, cg * 512:(cg + 1) * 512], op=ALU.mult),
                    reads=[Rbank[bk], R_bc], writes=[R_m4t[mt]])
                S.op("dve", lambda e, mt=mt, i=i, cg=cg: e.tensor_tensor(
                    out=x1[:, i, cg * 512:(cg + 1) * 512], in0=x1[:, i, cg * 512:(cg + 1) * 512], in1=m4t[mt][:, :], op=ALU.add),
                    reads=[R_m4t[mt], R_x1[i]], awrites=[R_x1[i]])
        dump("x1", lambda: x1[:, :, :], [128, 8, 2048], F32, R_x1)

        S.op("dve", lambda e: e.scalar_tensor_tensor(out=gm2T[:, :], in0=modT[:, 64:80], scalar=1.0,
                                                     in1=cst[:, O_GFFN:O_GFFN + 16], op0=ALU.add, op1=ALU.mult),
             reads=[R_mod[4], R_cst], writes=[R_gm2])
        R_fxn = [Res("fxn0"), Res("fxn1")]
        R_h2 = [Res("h2T%d" % i) for i in range(8)]
        sh2T = modT[:, 48:64]
        R_junk2 = Res("junk2")

        def f1_A(i):
            norm_A(x1[:, i, :], 128, None, R_x1[i], fxn[i % 2], R_fxn[i % 2], 17 + i, fxn[i % 2], R_fxn[i % 2], x_is_sbuf=True)

        def f1_B(i):
            norm_B(128, fxn[i % 2], R_fxn[i % 2], gm2T, R_gm2, sh2T, R_mod[3],
                   lambda kc, i=i: h2T[:, kc, i * 128:(i + 1) * 128], R_h2[i], 4 * (i % 2))
        f1_A(0)
        for i in range(8):
            if i + 1 < 8:
                f1_A(i + 1)
            f1_B(i)
        dump("h2T", lambda: h2T[:, :, :], [128, 16, 1024], BF16, R_h2)
        build_bc(5)
        S.barrier()

        R_gT = [[Res("gT%d_%d" % (fc, th)) for th in range(2)] for fc in range(16)]
        R_sa = [Res("sa0"), Res("sa1")]
        R_dt = [Res("dt0"), Res("dt1")]
        poolGa, poolGb, poolD = [0, 1], [2, 3], [4, 5, 6, 7]
        ga_i, gb_i, pd_i = [0], [0], [0]
        sa_i = 0
        dt_i = 0
        nout = 0
        for pi, (u0, u1) in enumerate(PHASES):
            nfc = 2 * (u1 - u0)
            for u in range(u0, u1):
                buf, Rb, _ = ring_next(("gu", u))
                for fcl in range(2):
                    fc = 2 * (u - u0) + fcl
                    for th in range(2):
                        ba = nextbank(poolGa, ga_i)
                        bb = nextbank(poolGb, gb_i)

                        def fa(e, ba=ba, fcl=fcl, th=th, buf=buf):
                            for kc in range(16):
                                ins = e.matmul(banks[ba][:, :], lhsT=buf[:, kc, fcl * 128:(fcl + 1) * 128],
                                               rhs=h2T[:, kc, th * 512:(th + 1) * 512], start=(kc == 0), stop=(kc == 15))
                            return ins

                        def fb_(e, bb=bb, fcl=fcl, th=th, buf=buf):
                            for kc in range(16):
                                ins = e.matmul(banks[bb][:, :], lhsT=buf[:, kc, 256 + fcl * 128:256 + (fcl + 1) * 128],
                                               rhs=h2T[:, kc, th * 512:(th + 1) * 512], start=(kc == 0), stop=(kc == 15))
                            return ins
                        S.op("pe", fa, reads=[Rb] + R_h2[4 * th:4 * th + 4], writes=[Rbank[ba]])
                        S.op("pe", fb_, reads=[Rb] + R_h2[4 * th:4 * th + 4], writes=[Rbank[bb]])
                        if fcl == 0 and th == 0:
                            ring_prefetch()
                        si = sa_i % 2
                        sa_i += 1
                        S.op("act", lambda e, ba=ba, si=si: e.activation(out=sab[si][:, :], in_=banks[ba][:, :], func=AF.Silu),
                             reads=[Rbank[ba]], writes=[R_sa[si]])
                        S.op("dve", lambda e, bb=bb, si=si, fc=fc, th=th: e.tensor_tensor(
                            out=gT[:, fc, th * 512:(th + 1) * 512], in0=sab[si][:, :], in1=banks[bb][:, :], op=ALU.mult),
                            reads=[R_sa[si], Rbank[bb]], writes=[R_gT[fc][th]])
            for dg in range(4):
                buf, Rb, _ = ring_next(("wd", u0, dg))
                for i in range(8):
                    bk = nextbank(poolD, pd_i)

                    def fd(e, bk=bk, i=i, buf=buf, nfc=nfc):
                        for fc in range(nfc):
                            ins = e.matmul(banks[bk][:, :], lhsT=gT[:, fc, i * 128:(i + 1) * 128], rhs=buf[:, fc, :],
                                           start=(fc == 0), stop=(fc == nfc - 1))
                        return ins
                    S.op("pe", fd, reads=[Rb] + [R_gT[fc][i // 4] for fc in range(nfc)], writes=[Rbank[bk]])
                    if i == 0:
                        ring_prefetch()
                    di = dt_i % 2
                    dt_i += 1
                    S.op("dve", lambda e, bk=bk, di=di, dg=dg: e.tensor_tensor(
                        out=dtb[di][:, :], in0=banks[bk][:, :], in1=bc[:, dg * 512:(dg + 1) * 512], op=ALU.mult),
                        reads=[Rbank[bk], R_bc], writes=[R_dt[di]])
                    S.op("dve", lambda e, di=di, i=i, dg=dg: e.tensor_tensor(
                        out=x1[:, i, dg * 512:(dg + 1) * 512], in0=x1[:, i, dg * 512:(dg + 1) * 512], in1=dtb[di][:, :],
                        op=ALU.add), reads=[R_dt[di], R_x1[i]], awrites=[R_x1[i]])
                    if pi == len(PHASES) - 1 and dg == 3:
                        S.op("sp", lambda e, i=i: e.dma_start(out=out_d[i * 128:(i + 1) * 128, :], in_=x1[:, i, :]),
                             reads=[R_x1[i]], dma="D_out")
                        nout += 1
        S.final_wait("sp", "D_out")
        if S.cnt["D_dbg"]:
            S.final_wait("sp", "D_dbg")

        block = stack.enter_context(nc.Block())

        @block.tensor
        def _(e):
            for r in S.prog["pe"]:
                r(e)

        @block.scalar
        def _(e):
            for r in S.prog["act"]:
                r(e)

        @block.vector
        def _(e):
            for r in S.prog["dve"]:
                r(e)

        @block.gpsimd
        def _(e):
            for r in S.prog["pool"]:
                r(e)

        @block.sync
        def _(e):
            for r in S.prog["sp"]:
                r(e)
    return nc, dbg


def host_inputs(x, c, positions, w_ada, b_ada, g_mix_norm, w_in, g_q, g_k, w_pool, pool_scale, w_out,
                g_ffn_norm, w_gate, w_up, w_down):
    f32 = np.float32
    x = np.asarray(x, f32)
    c = np.asarray(c, f32)
    positions = np.asarray(positions, np.int32)
    shared = {
        "w_ada": np.ascontiguousarray(np.asarray(w_ada, f32)[0]),
        "w_in": np.ascontiguousarray(np.asarray(w_in, f32)[0]),
        "w_pool": np.ascontiguousarray(np.asarray(w_pool, f32)[0]),
        "w_out": np.ascontiguousarray(np.asarray(w_out, f32)[0]),
        "w_gate": np.ascontiguousarray(np.asarray(w_gate, f32)[0]),
        "w_up": np.ascontiguousarray(np.asarray(w_up, f32)[0]),
        "w_down": np.ascontiguousarray(np.asarray(w_down, f32)[0]),
    }
    b_ada = np.asarray(b_ada, f32)[0]
    gmix = np.asarray(g_mix_norm, f32)[0]
    gffn = np.asarray(g_ffn_norm, f32)[0]
    gq = np.asarray(g_q, f32)[0]
    gk = np.asarray(g_k, f32)[0]
    psc = np.asarray(pool_scale, f32)[0]
    inv_freq = (10000.0 ** (-np.arange(0, DH, 2, dtype=f32) / f32(DH))).astype(f32)
    in_maps = []
    for core in range(8):
        b, p = core // 2, core % 2
        own, oth = OWNS[p], OWNS[1 - p]
        rows_own = np.concatenate([np.arange(g * 256, (g + 1) * 256) for g in own])
        rows_oth = np.concatenate([np.arange(g * 256, (g + 1) * 256) for g in oth])
        x_halo = np.zeros((64, D), f32)
        hv = np.zeros((4, 16), f32)
        for s_, g in enumerate(own):
            if g > 0:
                x_halo[s_ * 16:(s_ + 1) * 16] = x[b, g * 256 - 16:g * 256]
                hv[s_] = 1.0
        cst = np.zeros((128, NCST), f32)
        cst[:, O_C:O_C + 16] = c[b].reshape(16, 128).T
        cst[:, O_BADA:O_BADA + 96] = b_ada.reshape(96, 128).T
        cst[:, O_GMIX:O_GMIX + 16] = gmix.reshape(16, 128).T
        cst[:, O_GFFN:O_GFFN + 16] = gffn.reshape(16, 128).T
        cst[:, O_GQ:O_GQ + 128] = gq[None, :]
        cst[:, O_GK:O_GK + 128] = gk[None, :]
        cst[:, O_PSC:O_PSC + 8] = psc.reshape(8, 128).T
        cst[:, O_IFR:O_IFR + 64] = inv_freq[None, :]
        glob = own + oth
        alw = np.zeros((4, 8), f32)
        dg = np.zeros((4, 8), f32)
        for s_ in range(4):
            for kb in range(8):
                alw[s_, kb] = 1.0 if glob[kb] < own[s_] else 0.0
                dg[s_, kb] = 1.0 if kb == s_ else 0.0
        cst[:, O_ALW:O_ALW + 32] = alw.reshape(1, 32)
        cst[:, O_DIAG:O_DIAG + 32] = dg.reshape(1, 32)
        cst[:, O_HV:O_HV + 64] = hv.reshape(1, 64)
        cst[:, O_ID:O_ID + 128] = np.eye(128, dtype=f32)
        cstb = np.zeros((128, NCSTB), f32)
        cstb[:, OB_ID:OB_ID + 128] = np.eye(128, dtype=f32)
        kp = np.arange(128)[:, None, None]
        kt = np.arange(2)[None, :, None]
        q = np.arange(256)[None, None, :]
        cstb[:, OB_TRI:OB_TRI + 512] = np.where(kt * 128 + kp <= q, 0.0, NEG).astype(f32).reshape(128, 512)
        eo = np.zeros((8, 8, 128), f32)
        for j in range(8):
            eo[j, j, :] = 1.0
        cstb[0:8, OB_EOH:OB_EOH + 1024] = eo.reshape(8, 1024)
        pos_all = np.concatenate([positions[b, rows_own], positions[b, rows_oth]])
        pos_l = np.ascontiguousarray(pos_all.reshape(16, 128).T).astype(np.int32)
        t_glob = rows_own.astype(np.int64)
        invcnt = np.stack([1.0 / np.minimum(t_glob + 1, w).astype(f32) for w in POOL_WINDOWS]).astype(f32)
        invcnt_b = np.ascontiguousarray(np.broadcast_to(invcnt[:, None, :], (4, 128, 1024)))
        m = {
            "x_own": np.ascontiguousarray(x[b, rows_own]),
            "x_oth": np.ascontiguousarray(x[b, rows_oth]),
            "x_halo": x_halo,
            "cst": cst, "cstb": cstb, "pos_l": pos_l, "invcnt_b": invcnt_b,
        }
        m.update(shared)
        in_maps.append(m)
    return in_maps


_CACHE = {}


def kernel(x, c, positions, w_ada, b_ada, g_mix_norm, w_in, g_q, g_k, w_pool, pool_scale, w_out,
           g_ffn_norm, w_gate, w_up, w_down):
    in_maps = host_inputs(x, c, positions, w_ada, b_ada, g_mix_norm, w_in, g_q, g_k, w_pool, pool_scale,
                          w_out, g_ffn_norm, w_gate, w_up, w_down)
    key = tuple(DEBUG)
    if key not in _CACHE:
        _CACHE[key] = build_program(DEBUG)
    nc, dbg = _CACHE[key]
    res = run_bass_kernel_spmd(nc, in_maps, core_ids=list(range(8)))
    out = np.empty((NB, SEQ, D), np.float32)
    for core in range(8):
        b, p = core // 2, core % 2
        r = res.results[core]["out"]
        for s_, g in enumerate(OWNS[p]):
            out[b, g * 256:(g + 1) * 256] = r[s_ * 256:(s_ + 1) * 256]
    if DEBUG:
        kernel.last_results = res.results
    return out
```
